# Optimizing a Trainium2 kernel written in Bass

```python
import jax
import jax.numpy as jnp
from jax import lax
import numpy as np

D_MODEL = 1024
BATCH = 4
SEQ = 8192
DEPTH = 2

HEAD_DIM = 64
ROT_DIM = HEAD_DIM // 4
ROPE_THETA = 500000.0
BLOCK = 128
RMS_EPS = 1e-6
LN_EPS = 1e-5

A_HEADS = (D_MODEL // 2) // HEAD_DIM
A_KV_HEADS = 2
A_WINDOW = 128
CONV_CH = D_MODEL // 2
CONV_WIDTH = 31
C_HEADS = (D_MODEL // 2) // HEAD_DIM
DILATED_PAIRS = ((128, 1), (512, 4), (2048, 16))
D_CH = D_MODEL // 2
D_GROUPS = D_CH // HEAD_DIM
CHUNK = 128
D_FF = ((8 * D_MODEL // 3 + 255) // 256) * 256

A_Q = A_HEADS * HEAD_DIM
A_KV = A_KV_HEADS * HEAD_DIM
EVEN_IN = A_Q + 2 * A_KV + 2 * CONV_CH
EVEN_OUT = A_Q + CONV_CH
C_W = C_HEADS * HEAD_DIM
ODD_IN = 3 * C_W + 2 * D_CH
ODD_OUT = C_W + D_CH
N_EVEN = (DEPTH + 1) // 2
N_ODD = DEPTH // 2

kernel_name = 'hybrid_swa_sink_conformer_dilated_gmlp'


def rms_norm(x, g):
    xf = x.astype(jnp.float32)
    y = xf * lax.rsqrt(jnp.mean(xf * xf, axis=-1, keepdims=True) + RMS_EPS)
    return (y * g.astype(jnp.float32)).astype(x.dtype)


def layer_norm(x, g, b):
    xf = x.astype(jnp.float32)
    mu = jnp.mean(xf, axis=-1, keepdims=True)
    xc = xf - mu
    var = jnp.mean(xc * xc, axis=-1, keepdims=True)
    y = xc * lax.rsqrt(var + LN_EPS) * g.astype(jnp.float32) + b.astype(jnp.float32)
    return y.astype(x.dtype)


def rotary(x, pos):
    half = ROT_DIM // 2
    inv_freq = ROPE_THETA ** (-jnp.arange(half, dtype=jnp.float32) * (2.0 / ROT_DIM))
    ang = pos.astype(jnp.float32)[:, None] * inv_freq[None, :]
    cos = jnp.cos(ang)[None, :, None, :]
    sin = jnp.sin(ang)[None, :, None, :]
    xr = x[..., :ROT_DIM].astype(jnp.float32)
    x1, x2 = xr[..., :half], xr[..., half:]
    rot = jnp.concatenate([x1 * cos - x2 * sin, x2 * cos + x1 * sin], axis=-1).astype(x.dtype)
    return jnp.concatenate([rot, x[..., ROT_DIM:]], axis=-1)


def band_attention(q, k, v, max_dist, sink=None):
    B, S, Hq, hd = q.shape
    Hkv = k.shape[2]
    G = Hq // Hkv
    n = S // BLOCK
    qb = q.reshape(B, n, BLOCK, Hkv, G, hd)
    kb = k.reshape(B, n, BLOCK, Hkv, hd)
    vb = v.reshape(B, n, BLOCK, Hkv, hd)
    prev = lambda t: jnp.pad(t, ((0, 0), (1, 0), (0, 0), (0, 0), (0, 0)))[:, :-1]
    kk = jnp.concatenate([prev(kb), kb], axis=2)
    vv = jnp.concatenate([prev(vb), vb], axis=2)
    s = jnp.einsum('bnqhgd,bnjhd->bnhgqj', qb, kk,
                   preferred_element_type=jnp.float32) * (hd ** -0.5)
    qi = jnp.arange(BLOCK)[:, None]
    kj = jnp.arange(2 * BLOCK)[None, :]
    dist = qi + BLOCK - kj
    key_pos = jnp.arange(n)[:, None, None] * BLOCK + kj[None] - BLOCK
    valid = (dist >= 0)[None] & (dist <= max_dist)[None] & (key_pos >= 0)
    s = jnp.where(valid[None, :, None, None], s, -jnp.inf)
    m = jnp.max(s, axis=-1)
    if sink is not None:
        sink_b = sink.astype(jnp.float32).reshape(Hkv, G)[None, None, :, :, None]
        m = jnp.maximum(m, sink_b)
    p = jnp.exp(s - m[..., None])
    l = jnp.sum(p, axis=-1)
    if sink is not None:
        l = l + jnp.exp(sink_b - m)
    o = jnp.einsum('bnhgqj,bnjhd->bnqhgd', p.astype(v.dtype), vv,
                   preferred_element_type=jnp.float32)
    o = o / jnp.transpose(l, (0, 1, 4, 2, 3))[..., None]
    lse = jnp.transpose(m + jnp.log(l), (0, 1, 4, 2, 3)).reshape(B, S, Hq)
    return o.reshape(B, S, Hq, hd).astype(q.dtype), lse


def dilated_window_attention(q, k, v, window, dilation):
    B, S, H, hd = q.shape
    span = dilation * BLOCK
    s_pad = -(-S // span) * span
    sub = s_pad // dilation

    def fold(t):
        t = jnp.pad(t, ((0, 0), (0, s_pad - S), (0, 0), (0, 0)))
        t = t.reshape(B, sub, dilation, t.shape[2], hd)
        return jnp.transpose(t, (0, 2, 1, 3, 4)).reshape(B * dilation, sub, t.shape[3], hd)

    o, lse = band_attention(fold(q), fold(k), fold(v), window // dilation)
    o = jnp.transpose(o.reshape(B, dilation, sub, H, hd), (0, 2, 1, 3, 4)).reshape(B, s_pad, H, hd)
    lse = jnp.transpose(lse.reshape(B, dilation, sub, H), (0, 2, 1, 3)).reshape(B, s_pad, H)
    return o[:, :S], lse[:, :S]


def causal_depthwise_conv(x, w, b):
    C = x.shape[-1]
    y = lax.conv_general_dilated(
        x, w[:, None, :].astype(x.dtype), window_strides=(1,),
        padding=[(CONV_WIDTH - 1, 0)], dimension_numbers=('NWC', 'WIO', 'NWC'),
        feature_group_count=C)
    return y + b.astype(x.dtype)


def even_mixer(h, w_in, sinks, conv_w, conv_b, ln_g, ln_b, w_out, pos):
    B, S, _ = h.shape
    proj = jnp.einsum('bsd,de->bse', h, w_in)
    q, k, v, glu = jnp.split(proj, [A_Q, A_Q + A_KV, A_Q + 2 * A_KV], axis=-1)
    q = rotary(q.reshape(B, S, A_HEADS, HEAD_DIM), pos)
    k = rotary(k.reshape(B, S, A_KV_HEADS, HEAD_DIM), pos)
    v = v.reshape(B, S, A_KV_HEADS, HEAD_DIM)
    a, _ = band_attention(q, k, v, A_WINDOW - 1, sinks)
    a = a.reshape(B, S, A_Q)
    g_a, g_b = jnp.split(glu, 2, axis=-1)
    c = g_a * jax.nn.sigmoid(g_b)
    c = causal_depthwise_conv(c, conv_w, conv_b)
    c = jax.nn.silu(layer_norm(c, ln_g, ln_b))
    return jnp.einsum('bse,ed->bsd', jnp.concatenate([a, c], axis=-1), w_out)


def odd_mixer(h, w_in, sgu_ln_g, sgu_ln_b, spatial_w, spatial_b, w_out, pos):
    B, S, _ = h.shape
    proj = jnp.einsum('bsd,de->bse', h, w_in)
    q, k, v, z = jnp.split(proj, [C_W, 2 * C_W, 3 * C_W], axis=-1)
    q = rotary(q.reshape(B, S, C_HEADS, HEAD_DIM), pos)
    k = rotary(k.reshape(B, S, C_HEADS, HEAD_DIM), pos)
    v = v.reshape(B, S, C_HEADS, HEAD_DIM)
    outs, lses = [], []
    for window, dilation in DILATED_PAIRS:
        o_r, lse_r = dilated_window_attention(q, k, v, window, dilation)
        outs.append(o_r)
        lses.append(lse_r)
    alpha = jax.nn.softmax(jnp.stack(lses, axis=0), axis=0)
    c_out = jnp.einsum('rbsh,rbshd->bshd', alpha, jnp.stack(outs, axis=0).astype(jnp.float32))
    c_out = c_out.astype(h.dtype).reshape(B, S, C_W)
    z = jax.nn.gelu(z)
    u, g = jnp.split(z, 2, axis=-1)
    g = layer_norm(g, sgu_ln_g, sgu_ln_b).reshape(B, S // CHUNK, CHUNK, D_GROUPS, HEAD_DIM)
    causal = jnp.tril(jnp.ones((CHUNK, CHUNK), dtype=bool))
    w_s = jnp.where(causal[None], spatial_w, 0).astype(g.dtype)
    mixed = jnp.einsum('gts,bcsgd->bctgd', w_s, g) + spatial_b.T.astype(g.dtype)[None, None, :, :, None]
    d_out = u * mixed.reshape(B, S, D_CH)
    return jnp.einsum('bse,ed->bsd', jnp.concatenate([c_out, d_out], axis=-1), w_out)


def swiglu(h, w_gate, w_up, w_down):
    gate = jnp.einsum('bsd,df->bsf', h, w_gate)
    up = jnp.einsum('bsd,df->bsf', h, w_up)
    return jnp.einsum('bsf,fd->bsd', jax.nn.silu(gate) * up, w_down)


def setup_inputs(seed: int = 0) -> dict:
    key = jax.random.key(seed)
    ks = jax.random.split(key, 21)
    f32 = jnp.float32

    def nrm(k, shape, scale):
        return jax.random.normal(k, shape, f32) * scale

    return {
        'x': nrm(ks[0], (BATCH, SEQ, D_MODEL), 1.0),
        'ev_norm_g': 1.0 + nrm(ks[1], (N_EVEN, D_MODEL), 0.02),
        'ev_w_in': nrm(ks[2], (N_EVEN, D_MODEL, EVEN_IN), D_MODEL ** -0.5),
        'ev_sinks': nrm(ks[3], (N_EVEN, A_HEADS), 0.5),
        'ev_conv_w': nrm(ks[4], (N_EVEN, CONV_WIDTH, CONV_CH), CONV_WIDTH ** -0.5),
        'ev_conv_b': nrm(ks[5], (N_EVEN, CONV_CH), 0.02),
        'ev_conv_ln_g': 1.0 + nrm(ks[6], (N_EVEN, CONV_CH), 0.02),
        'ev_conv_ln_b': nrm(ks[7], (N_EVEN, CONV_CH), 0.02),
        'ev_w_out': nrm(ks[8], (N_EVEN, EVEN_OUT, D_MODEL), EVEN_OUT ** -0.5),
        'od_norm_g': 1.0 + nrm(ks[9], (N_ODD, D_MODEL), 0.02),
        'od_w_in': nrm(ks[10], (N_ODD, D_MODEL, ODD_IN), D_MODEL ** -0.5),
        'od_sgu_ln_g': 1.0 + nrm(ks[11], (N_ODD, D_CH), 0.02),
        'od_sgu_ln_b': nrm(ks[12], (N_ODD, D_CH), 0.02),
        'od_spatial_w': nrm(ks[13], (N_ODD, D_GROUPS, CHUNK, CHUNK), CHUNK ** -0.5),
        'od_spatial_b': 1.0 + nrm(ks[14], (N_ODD, D_GROUPS, CHUNK), 0.02),
        'od_w_out': nrm(ks[15], (N_ODD, ODD_OUT, D_MODEL), ODD_OUT ** -0.5),
        'ffn_norm_g': 1.0 + nrm(ks[16], (DEPTH, D_MODEL), 0.02),
        'ffn_w_gate': nrm(ks[17], (DEPTH, D_MODEL, D_FF), D_MODEL ** -0.5),
        'ffn_w_up': nrm(ks[18], (DEPTH, D_MODEL, D_FF), D_MODEL ** -0.5),
        'ffn_w_down': nrm(ks[19], (DEPTH, D_FF, D_MODEL), D_FF ** -0.5),
        'final_norm_g': 1.0 + nrm(ks[20], (D_MODEL,), 0.02),
    }


def reference(x, ev_norm_g, ev_w_in, ev_sinks, ev_conv_w, ev_conv_b, ev_conv_ln_g,
              ev_conv_ln_b, ev_w_out, od_norm_g, od_w_in, od_sgu_ln_g, od_sgu_ln_b,
              od_spatial_w, od_spatial_b, od_w_out, ffn_norm_g, ffn_w_gate, ffn_w_up,
              ffn_w_down, final_norm_g):
    pos = jnp.arange(x.shape[1], dtype=jnp.int32)
    h = x
    for layer in range(DEPTH):
        i = layer // 2
        if layer % 2 == 0:
            h = h + even_mixer(rms_norm(h, ev_norm_g[i]), ev_w_in[i], ev_sinks[i],
                               ev_conv_w[i], ev_conv_b[i], ev_conv_ln_g[i],
                               ev_conv_ln_b[i], ev_w_out[i], pos)
        else:
            h = h + odd_mixer(rms_norm(h, od_norm_g[i]), od_w_in[i], od_sgu_ln_g[i],
                              od_sgu_ln_b[i], od_spatial_w[i], od_spatial_b[i],
                              od_w_out[i], pos)
        h = h + swiglu(rms_norm(h, ffn_norm_g[layer]), ffn_w_gate[layer],
                       ffn_w_up[layer], ffn_w_down[layer])
    return rms_norm(h, final_norm_g)
```

```python
import numpy as np
import concourse.bass as bass
import concourse.mybir as mybir
from concourse.bass_utils import run_bass_kernel_spmd

F32 = mybir.dt.float32
BF16 = mybir.dt.bfloat16
AF = mybir.ActivationFunctionType
ALU = mybir.AluOpType

T = 512
D = 1024
KC = 8
DFF = 2816
FC = 22
NEG = -30000.0
NSLOT = 6
CAST_AHEAD = 16


class Prog:
    def __init__(self, nc):
        self.nc = nc
        self.ops = []
        self.state = {}
        self.last_dma = {}
        self.phase = ''
        self.names = {}

    def add(self, eng, fn, reads=(), writes=(), dma=None):
        idx = len(self.ops)
        sem = dma if dma else eng
        deps = {}
        reads = list(reads)
        writes = list(writes)
        writes += [k for k in reads if isinstance(k, tuple) and k[0] == "ps"]
        reads = [k for k in reads if not (isinstance(k, tuple) and k[0] == "ps")]

        def need(d):
            for s, i in d.items():
                if deps.get(s, -1) < i:
                    deps[s] = i

        for k in reads:
            st = self.state.setdefault(k, {"w": {}, "r": {}})
            need(st["w"])
        for k in writes:
            st = self.state.setdefault(k, {"w": {}, "r": {}})
            need(st["w"])
            need(st["r"])
        for k in reads:
            self.state[k]["r"][sem] = idx
        for k in writes:
            st = self.state[k]
            st["w"] = {sem: idx}
            st["r"] = {}
        if eng == "pe" and not dma:
            deps.pop("pe", None)
        if dma:
            prev = self.last_dma.get(sem)
            if prev is not None and deps.get(sem, -1) < prev:
                deps[sem] = prev
            self.last_dma[sem] = idx
        self.ops.append(dict(eng=eng, fn=fn, sem=sem, deps=deps, dma=bool(dma), signal=False, tok=None, phase=self.phase))
        return idx

    def emit(self):
        nc = self.nc
        ops = self.ops
        for op in ops:
            for s, i in op["deps"].items():
                ops[i]["signal"] = True
        cnt = {}
        for op in ops:
            s = op["sem"]
            if op["dma"]:
                cnt[s] = cnt.get(s, 0) + 16
                op["tok"] = cnt[s]
            elif op["signal"]:
                cnt[s] = cnt.get(s, 0) + 1
                op["tok"] = cnt[s]
        sems = {s: nc.alloc_semaphore("sem_" + str(s)) for s in sorted(set(op["sem"] for op in ops))}

        def run_engine(ename):
            def body(eng):
                known = {}
                for op in ops:
                    if op["eng"] != ename:
                        continue
                    for s, i in op["deps"].items():
                        v = ops[i]["tok"]
                        if known.get(s, 0) < v:
                            eng.wait_ge(sems[s], v)
                            known[s] = v
                    ins = op["fn"](eng)
                    try:
                        self.names[ins.ins.name] = (op["phase"], ename)
                    except Exception:
                        pass
                    if op["dma"]:
                        ins.then_inc(sems[op["sem"]], 16)
                    elif op["signal"]:
                        ins.then_inc(sems[op["sem"]], 1)
                for s in sorted(set(op["sem"] for op in ops if op["eng"] == ename and op["dma"])):
                    if known.get(s, 0) < cnt[s]:
                        eng.wait_ge(sems[s], cnt[s])
                        known[s] = cnt[s]
            return body

        with nc.Block() as block:
            block.sync(run_engine("sp"))
            block.tensor(run_engine("pe"))
            block.scalar(run_engine("act"))
            block.vector(run_engine("dve"))
            block.gpsimd(run_engine("pool"))


def _swap_cols(base):
    return list(range(base + 8, base + 16)) + list(range(base, base + 8)) + list(range(base + 16, base + 64))


def _unit_table():
    names = []
    for i in range(4):
        names.append(("q0", i))
    names.append(("k0",))
    for i in range(4):
        names.append(("glu", i))
    names.append(("v0",))
    for cc in range(4):
        names.append(("conv", cc, 0))
        names.append(("conv", cc, 1))
    for o in range(4):
        names.append(("out", 0, o))
    for l in range(2):
        for i in range(11):
            names.append(("gate", l, i))
            names.append(("up", l, i))
        for oc in range(8):
            names.append(("down", l, oc, 0))
            names.append(("down", l, oc, 1))
    for i in range(4):
        names.append(("k1", i))
    for i in range(2):
        names.append(("v1", i))
    for i in range(4):
        names.append(("q1", i))
    for i in range(2):
        names.append(("u1", i))
    for i in range(2):
        names.append(("g1", i))
    for o in range(4):
        names.append(("out", 1, o))
    return names


UNIT_NAMES = _unit_table()
UNIT_IDX = {n: i for i, n in enumerate(UNIT_NAMES)}
NU = len(UNIT_NAMES)


def _std_unit(W, cols, rows=None):
    cols = list(cols)
    if len(cols) < 256:
        cols = cols + [cols[0]] * (256 - len(cols))
    Wc = W[:, cols]
    if rows is not None:
        Wc = Wc[rows, :]
    return np.ascontiguousarray(Wc.reshape(8, 128, 256).transpose(1, 0, 2).reshape(128, 2048))


def _build_wall(inp):
    wall = np.zeros((NU, 128, 2048), np.float32)
    w_in0 = inp["ev_w_in"][0]
    w_out0 = inp["ev_w_out"][0]
    w_in1 = inp["od_w_in"][0]
    w_out1 = inp["od_w_out"][0]
    conv_w = inp["ev_conv_w"][0]
    for i in range(4):
        ha, hb = i * 64, (4 + i) * 64
        cols = list(range(ha, ha + 64)) + list(range(hb, hb + 64)) + _swap_cols(ha) + _swap_cols(hb)
        wall[UNIT_IDX[("q0", i)]] = _std_unit(w_in0, cols)
    cols = list(range(512, 640)) + _swap_cols(512) + _swap_cols(576)
    wall[UNIT_IDX[("k0",)]] = _std_unit(w_in0, cols)
    for i in range(4):
        cols = list(range(768 + 128 * i, 768 + 128 * i + 128)) + list(range(1280 + 128 * i, 1280 + 128 * i + 128))
        wall[UNIT_IDX[("glu", i)]] = _std_unit(w_in0, cols)
    wall[UNIT_IDX[("v0",)]] = _std_unit(w_in0, list(range(640, 768)))
    ar = np.arange(128)
    for cc in range(4):
        for half in range(2):
            u = np.zeros((128, 16, 128), np.float32)
            taps = range(16) if half == 0 else range(16, 31)
            for jj, tap in enumerate(taps):
                u[ar, jj, ar] = conv_w[tap, cc * 128:(cc + 1) * 128]
            wall[UNIT_IDX[("conv", cc, half)]] = u.reshape(128, 2048)
    rows0 = []
    for c in range(4):
        rows0 += list(range(c * 64, c * 64 + 64)) + list(range((4 + c) * 64, (4 + c) * 64 + 64))
    rows0 += list(range(512, 1024))
    for o in range(4):
        wall[UNIT_IDX[("out", 0, o)]] = _std_unit(w_out0, range(256 * o, 256 * o + 256), rows=rows0)
        wall[UNIT_IDX[("out", 1, o)]] = _std_unit(w_out1, range(256 * o, 256 * o + 256))
    for l in range(2):
        wg, wu, wd = inp["ffn_w_gate"][l], inp["ffn_w_up"][l], inp["ffn_w_down"][l]
        for i in range(11):
            wall[UNIT_IDX[("gate", l, i)]] = _std_unit(wg, range(256 * i, 256 * i + 256))
            wall[UNIT_IDX[("up", l, i)]] = _std_unit(wu, range(256 * i, 256 * i + 256))
        for oc in range(8):
            for half in range(2):
                blk = wd[half * 1408:(half + 1) * 1408, oc * 128:(oc + 1) * 128]
                u = blk.reshape(11, 128, 128).transpose(1, 0, 2).reshape(128, 1408)
                wall[UNIT_IDX[("down", l, oc, half)], :, :1408] = u
    for i in range(4):
        ha, hb = 128 * i, 128 * i + 64
        cols = list(range(ha, ha + 128)) + _swap_cols(ha) + _swap_cols(hb)
        wall[UNIT_IDX[("q1", i)]] = _std_unit(w_in1, cols)
        cols = list(range(512 + ha, 512 + ha + 128)) + _swap_cols(512 + ha) + _swap_cols(512 + hb)
        wall[UNIT_IDX[("k1", i)]] = _std_unit(w_in1, cols)
    for i in range(2):
        wall[UNIT_IDX[("v1", i)]] = _std_unit(w_in1, range(1024 + 256 * i, 1024 + 256 * i + 256))
        wall[UNIT_IDX[("u1", i)]] = _std_unit(w_in1, range(1536 + 256 * i, 1536 + 256 * i + 256))
        wall[UNIT_IDX[("g1", i)]] = _std_unit(w_in1, range(2048 + 256 * i, 2048 + 256 * i + 256))
    return wall


def _fm(v):
    return np.ascontiguousarray(v.reshape(-1, 128).T)


VC = dict(g_ev=0, g_ffn0=8, g_od=16, g_ffn1=24, conv_b=32, ln_g=36, ln_b=40, sink=44)
NV = 52


def _build_consts(inp):
    vecs = np.zeros((128, NV), np.float32)
    vecs[:, 0:8] = _fm(inp["ev_norm_g"][0])
    vecs[:, 8:16] = _fm(inp["ffn_norm_g"][0])
    vecs[:, 16:24] = _fm(inp["od_norm_g"][0])
    vecs[:, 24:32] = _fm(inp["ffn_norm_g"][1])
    vecs[:, 32:36] = _fm(inp["ev_conv_b"][0])
    vecs[:, 36:40] = _fm(inp["ev_conv_ln_g"][0])
    vecs[:, 40:44] = _fm(inp["ev_conv_ln_b"][0])
    sk = inp["ev_sinks"][0]
    for c in range(4):
        vecs[64:128, 44 + c] = sk[c]
        vecs[0:64, 48 + c] = sk[4 + c]
    rep = np.zeros((128, 2048), np.float32)
    rep[:, 0:512] = inp["od_sgu_ln_g"][0][None, :]
    rep[:, 512:1024] = inp["od_sgu_ln_b"][0][None, :]
    rep[:, 1024:2048] = inp["final_norm_g"][None, :]
    sb = inp["od_spatial_b"][0]
    sbrep = np.zeros((128, 4, 128), np.float32)
    for c in range(4):
        sbrep[0:64, c, :] = sb[2 * c][None, :]
        sbrep[64:128, c, :] = sb[2 * c + 1][None, :]
    ws = inp["od_spatial_w"][0]
    tri = (np.arange(128)[:, None] <= np.arange(128)[None, :])
    wst = np.zeros((128, 8, 128), np.float32)
    for g in range(8):
        wst[:, g, :] = np.where(tri, ws[g].T, 0.0)
    kj = np.arange(128)[:, None]
    qi = np.arange(128)[None, :]
    m0 = np.zeros((128, 8, 128), np.float32)
    for bi, b in enumerate(range(-3, 5)):
        dl = 128 * b + qi - kj
        m0[:, bi, :] = ((dl >= 0) & (dl <= 127)).astype(np.float32)
    m1 = np.zeros((128, 23, 128), np.float32)
    for bi, b in enumerate(range(-3, 20)):
        dl = 128 * b + qi - kj
        m1[:, bi, :] = (((dl >= 0) & (dl <= 128)).astype(np.float32)
                        + ((dl >= 0) & (dl <= 512) & (dl % 4 == 0)).astype(np.float32)
                        + ((dl >= 0) & (dl <= 2048) & (dl % 16 == 0)).astype(np.float32))
    return dict(vecs=vecs, rep=rep, sbrep=sbrep.reshape(128, 512), wst=wst.reshape(128, 1024),
                m0=m0.reshape(128, 1024), m1=m1.reshape(128, 23 * 128),
                ident=np.eye(128, dtype=np.float32))


def _core_tables(s0, nhalo, nch):
    ntok = nch * T
    pos = s0 - nhalo * T + np.arange(ntok)
    posc = np.maximum(pos, 0).astype(np.float32)
    inv = (np.float32(500000.0) ** (-np.arange(8, dtype=np.float32) * np.float32(2.0 / 16))).astype(np.float32)
    ang = (posc[:, None] * inv[None, :]).astype(np.float32)
    cos = np.cos(ang).astype(np.float32)
    sin = np.sin(ang).astype(np.float32)
    cs = np.zeros((nch, 128, 2 * T), np.float32)
    cs[:, :, 0:T] = 1.0
    for p in range(128):
        i = p % 64
        if i < 16:
            f = i % 8
            cs[:, p, 0:T] = cos[:, f].reshape(nch, T)
            sg = -1.0 if i < 8 else 1.0
            cs[:, p, T:2 * T] = sg * sin[:, f].reshape(nch, T)
    kb = np.zeros((128, nch * 4), np.float32)
    for b in range(nch * 4):
        if s0 - nhalo * T + b * 128 < 0:
            kb[:, b] = NEG
    return cs, kb


def build_program(nown, nhalo):
    nch = nown + nhalo
    ntok = nch * T
    nc = bass.Bass("TRN2", target_bir_lowering=False)
    dI = lambda name, shape, dt=F32: nc.dram_tensor(name, list(shape), dt, kind="ExternalInput").ap()
    x_d = dI("x", [ntok, D])
    wall_d = dI("wall", [NU, 128, 2048])
    vecs_d = dI("vecs", [128, NV])
    rep_d = dI("rep", [128, 2048])
    sbrep_d = dI("sbrep", [128, 512])
    wst_d = dI("wst", [128, 1024])
    m0_d = dI("m0", [128, 1024])
    m1_d = dI("m1", [128, 23 * 128])
    ident_d = dI("ident", [128, 128])
    cs_d = dI("cs", [nch, 128, 2 * T])
    kb_d = dI("kbias", [128, nch * 4])
    out_d = nc.dram_tensor("out", [nown * T, D], F32, kind="ExternalOutput").ap()
    wsc_d = nc.dram_tensor("wsc", [NU, 128, 2048], BF16, kind="Internal").ap()

    SB = lambda name, shape, dt: nc.alloc_sbuf_tensor("sb_" + name, list(shape), dt)
    h = SB("h", [128, KC, T], F32)
    xn = SB("xn", [128, KC, T], BF16)
    hid = SB("hid", [128, FC, T], BF16)
    xin = [SB("xin%d" % i, [128, D], F32) for i in range(2)]
    gts = [xin[0][:, 0:512], xin[0][:, 512:1024], xin[1][:, 0:512], xin[1][:, 512:1024]]
    cst = [SB("cst%d" % i, [128, 2 * T], F32) for i in range(2)]
    QT = SB("QT", [128, 4, 2, T], BF16)
    K0T = SB("K0T", [128, 8 * 128], BF16)
    V0a = SB("V0a", [128, 8, 192], BF16)
    K1T = SB("K1T", [128, 4, 20 * 128], BF16)
    V1a = SB("V1a", [128, 20, 768], BF16)
    cpad = SB("cpad", [128, 4, 30 + T], BF16)
    cu = SB("cu", [128, 4, T], F32)
    gn = SB("gn", [128, 4, 512], BF16)
    wst = SB("wst", [128, 8, 128], BF16)
    sbrep = SB("sbrep", [128, 4, 128], F32)
    rep = SB("rep", [128, 2048], F32)
    m0 = SB("m0", [128, 1024], BF16)
    m1 = SB("m1", [128, 23 * 128], BF16)
    NPT = 3
    Pt = [SB("Pt%d" % i, [128, 2, T], BF16) for i in range(NPT)]
    NTMP = 5
    tmp = [SB("tmp%d" % i, [128, T], F32) for i in range(NTMP)]
    wring = SB("wring", [128, NSLOT, 2048], BF16)
    vecs = SB("vecs", [128, NV], F32)
    esink = SB("esink", [128, 8], F32)
    kbias = SB("kbias", [128, nch * 4], F32)
    ident = SB("ident", [128, 128], F32)
    ones = SB("ones", [128, 128], BF16)
    small = SB("small", [128, 64], F32)
    psall = nc.alloc_psum_tensor("psall", [128, 8 * T], F32)
    ps = [psall[:, i * T:(i + 1) * T] for i in range(8)]

    P = Prog(nc)
    st = dict(gen=0, acc=0, tmp=0, pt=0, wcnt=0, cast=0)

    def gen():
        st["gen"] = (st["gen"] + 1) % 4
        return st["gen"]

    def acc():
        st["acc"] = (st["acc"] + 1) % 4
        return 4 + st["acc"]

    def ntmp():
        st["tmp"] = (st["tmp"] + 1) % NTMP
        return st["tmp"]

    def npt():
        st["pt"] = (st["pt"] + 1) % NPT
        return st["pt"]

    cast_order = []
    seen = set()

    def cast_upto(n):
        while st["cast"] < min(n, len(cast_order)):
            u = cast_order[st["cast"]]
            P.add("pool", lambda e, u=u: e.dma_start(out=wsc_d[u], in_=wall_d[u], max_dma_last_dim=4096),
                  writes=[("wsc", u)], dma="cast%d" % (st["cast"] % 8))
            st["cast"] += 1

    def preregister(names):
        for n in names:
            u = UNIT_IDX[n]
            if u not in seen:
                seen.add(u)
                cast_order.append(u)

    l0_names = [("q0", i) for i in range(4)] + [("k0",)] + [("glu", i) for i in range(4)] + [("v0",)]
    l0_names += [("conv", cc, hf) for cc in range(4) for hf in range(2)] + [("out", 0, o) for o in range(4)]

    def ffn_names(l):
        r = []
        for i in range(11):
            r += [("gate", l, i), ("up", l, i)]
        for oc in range(8):
            r += [("down", l, oc, 0), ("down", l, oc, 1)]
        return r

    l1_halo = [("k1", i) for i in range(4)] + [("v1", i) for i in range(2)]
    l1_own = [("q1", i) for i in range(4)] + [("u1", i) for i in range(2)] + [("g1", i) for i in range(2)]
    l1_own += [("out", 1, o) for o in range(4)]
    preregister(l0_names + ffn_names(0) + l1_halo + l1_own + ffn_names(1))

    def unit_ahead(name):
        ncols = 1408 if name[0] == "down" else (1920 if (name[0] == "conv" and name[2] == 1) else 2048)
        u = UNIT_IDX[name]
        pos = cast_order.index(u)
        cast_upto(pos + 1 + CAST_AHEAD)
        slot = st["wcnt"] % NSLOT
        st["wcnt"] += 1
        P.add("sp", lambda e, u=u, slot=slot: e.dma_start(out=wring[:, slot, 0:ncols], in_=wsc_d[u][:, 0:ncols]),
              reads=[("wsc", u)], writes=[("w", slot)], dma="w%d" % slot)
        return slot

    W3 = lambda slot: wring[:, slot, :].rearrange("p (k n) -> p k n", n=256)
    W3d = lambda slot: wring[:, slot, 0:1408].rearrange("p (k n) -> p k n", n=128)
    W3c = lambda slot: wring[:, slot, :].rearrange("p (k n) -> p k n", n=128)

    def ld(eng, dst, src, key, sem):
        if eng == "pool":
            P.add(eng, lambda e: e.dma_start(out=dst, in_=src, max_dma_last_dim=4096), writes=[key], dma=sem)
        else:
            P.add(eng, lambda e: e.dma_start(out=dst, in_=src), writes=[key], dma=sem)

    ld("sp", vecs[:, :], vecs_d[:, :], "vecs", "s0")
    ld("sp", ident[:, :], ident_d[:, :], "ident", "s1")
    ld("sp", rep[:, :], rep_d[:, :], "rep", "s2")
    ld("sp", sbrep[:, :, :], sbrep_d.rearrange("p (c t) -> p c t", t=128), "sbrep", "s3")
    ld("sp", kbias[:, :], kb_d[:, :], "kbias", "s0")
    ld("pool", wst[:, :, :], wst_d.rearrange("p (g t) -> p g t", t=128), "wst", "c0")
    ld("pool", m0[:, :], m0_d[:, :], "m0", "c1")
    ld("pool", m1[:, :], m1_d[:, :], "m1", "c2")
    P.add("dve", lambda e: e.memset(ones[:, :], 1.0), writes=["ones"])
    P.add("dve", lambda e: e.memset(QT[:, :, :, :], 0.0), writes=[("qt", i) for i in range(4)])
    P.add("dve", lambda e: e.memset(K0T[:, :], 0.0), writes=[("k0", 0), ("k0", 1)])
    P.add("dve", lambda e: e.memset(V0a[:, :, :], 0.0), writes=[("v0", 0), ("v0", 1)])
    P.add("dve", lambda e: e.memset(V0a[:, :, 64:128], 1.0), writes=[("v0", 0), ("v0", 1)])
    P.add("dve", lambda e: e.memset(V1a[:, :, :], 0.0), writes=[("v1", i) for i in range(5)])
    v1v = V1a[:, :, :].rearrange("p s (c x) -> p s c x", x=192)
    for c in range(4):
        P.add("dve", lambda e, c=c: e.memset(v1v[:, :, c, 64:128], 1.0), writes=[("v1", i) for i in range(5)])
    P.add("dve", lambda e: e.memset(cpad[:, :, :], 0.0), writes=[("cpad", i) for i in range(4)])
    P.add("act", lambda e: e.activation(out=esink[:, :], in_=vecs[:, VC["sink"]:VC["sink"] + 8], func=AF.Exp),
          reads=["vecs"], writes=["esink"])

    def mm(out, lhsT, rhs, start, stop, reads, writes):
        P.add("pe", lambda e: e.matmul(out, lhsT=lhsT, rhs=rhs, start=start, stop=stop), reads=reads, writes=writes)

    def x_dma(j, b):
        gb = j * 4 + b
        buf = gb % 2
        P.add("sp", lambda e: e.dma_start(out=xin[buf][:, :], in_=x_d[gb * 128:(gb + 1) * 128, :]),
              writes=[("xin", buf), ("gt", 2 * buf), ("gt", 2 * buf + 1)], dma="x%d" % buf)

    def load_x(j, prefetched=False):
        P.phase = 'xT'
        for b in range(4):
            gb = j * 4 + b
            buf = gb % 2
            if not (prefetched and b < 2):
                x_dma(j, b)
            for half in range(2):
                bk = gen()
                for f in range(4):
                    fc = half * 4 + f
                    P.add("pe", lambda e, bk=bk, f=f, fc=fc, buf=buf: e.transpose(
                        ps[bk][:, f * 128:(f + 1) * 128], xin[buf][:, fc * 128:(fc + 1) * 128], ident[:, :]),
                        reads=[("xin", buf), ("gt", 2 * buf), ("gt", 2 * buf + 1), "ident"], writes=[("ps", bk)])
                eng = "act" if half == 0 else "dve"
                dst = h[:, half * 4:half * 4 + 4, b * 128:(b + 1) * 128]
                src = ps[bk][:, :].rearrange("p (f t) -> p f t", t=128)
                if eng == "act":
                    P.add("act", lambda e, dst=dst, src=src: e.copy(out=dst, in_=src),
                          reads=[("ps", bk)], writes=[("h", half * 4 + f) for f in range(4)])
                else:
                    P.add("dve", lambda e, dst=dst, src=src: e.tensor_copy(out=dst, in_=src),
                          reads=[("ps", bk)], writes=[("h", half * 4 + f) for f in range(4)])
        nb = acc()
        for fc in range(KC):
            sq_acc(nb, fc, fc == 0, fc == KC - 1)
        return nb

    def load_cs(j):
        buf = j % 2
        P.add("sp", lambda e: e.dma_start(out=cst[buf][:, :], in_=cs_d[j]), writes=[("cs", buf)], dma="cs%d" % buf)

    def rstd_from_bank(bk, scale, eps):
        t1 = ntmp()
        P.add("act", lambda e: e.activation(out=tmp[t1][:, :], in_=ps[bk][:, :], func=AF.Ln, bias=eps, scale=scale),
              reads=[("ps", bk)], writes=[("tmp", t1)])
        t2 = ntmp()
        P.add("act", lambda e: e.activation(out=tmp[t2][:, :], in_=tmp[t1][:, :], func=AF.Exp, scale=-0.5),
              reads=[("tmp", t1)], writes=[("tmp", t2)])
        return t2

    def sigmoid_to(t, src, scale, reads):
        P.add("act", lambda e: e.activation(out=tmp[t][:, :], in_=src, func=AF.Exp, scale=-scale),
              reads=reads, writes=[("tmp", t)])
        P.add("act", lambda e: e.activation(out=tmp[t][:, :], in_=tmp[t][:, :], func=AF.Ln, bias=1.0),
              reads=[("tmp", t)], writes=[("tmp", t)])
        P.add("act", lambda e: e.activation(out=tmp[t][:, :], in_=tmp[t][:, :], func=AF.Exp, scale=-1.0),
              reads=[("tmp", t)], writes=[("tmp", t)])

    GK2 = 2.0 * 0.7978845608028654

    def gelu_sig(bk):
        t = ntmp()
        P.add("act", lambda e: e.activation(out=tmp[t][:, :], in_=ps[bk][:, :], func=AF.Square),
              reads=[("ps", bk)], writes=[("tmp", t)])
        P.add("dve", lambda e: e.tensor_scalar(out=tmp[t][:, :], in0=tmp[t][:, :], scalar1=0.044715, scalar2=1.0,
                                               op0=ALU.mult, op1=ALU.add),
              reads=[("tmp", t)], writes=[("tmp", t)])
        P.add("dve", lambda e: e.tensor_tensor(out=tmp[t][:, :], in0=ps[bk][:, :], in1=tmp[t][:, :], op=ALU.mult),
              reads=[("ps", bk), ("tmp", t)], writes=[("tmp", t)])
        sigmoid_to(t, tmp[t][:, :], GK2, [("tmp", t)])
        return t

    def sq_part1(fc):
        P.add("act", lambda e: e.activation(out=xn[:, fc, :], in_=h[:, fc, :], func=AF.Square),
              reads=[("h", fc)], writes=[("xn", fc)])

    def sq_part2(nb, fc, first, last):
        mm(ps[nb][:, :], ones[:, :], xn[:, fc, :], first, last, [("xn", fc), "ones"], [("ps", nb)])

    def sq_acc(nb, fc, first, last):
        sq_part1(fc)
        sq_part2(nb, fc, first, last)

    def rmsnorm(gcol, nb):
        P.phase = 'norm'
        t1 = ntmp()
        P.add("act", lambda e: e.activation(out=tmp[t1][:, :], in_=ps[nb][:, :], func=AF.Ln, bias=1e-6, scale=1.0 / D),
              reads=[("ps", nb)], writes=[("tmp", t1)])
        P.add("act", lambda e: e.activation(out=ps[nb][:, :], in_=tmp[t1][:, :], func=AF.Exp, scale=-0.5),
              reads=[("tmp", t1)], writes=[("ps", nb)])
        for fc in range(KC):
            P.add("dve", lambda e, fc=fc: e.scalar_tensor_tensor(
                out=xn[:, fc, :], in0=h[:, fc, :], scalar=vecs[:, gcol + fc:gcol + fc + 1], in1=ps[nb][:, :],
                op0=ALU.mult, op1=ALU.mult),
                reads=[("h", fc), ("ps", nb), "vecs"], writes=[("xn", fc)])

    def proj_fm(slot, half):
        bk = gen()
        w = W3(slot)
        for kc in range(KC):
            mm(ps[bk][:, :], w[:, kc, half * 128:(half + 1) * 128], xn[:, kc, :], kc == 0, kc == KC - 1,
               [("w", slot), ("xn", kc)], [("ps", bk)])
        return bk

    def rotary_to(dst, dkey, bq, bs, csb):
        t1 = ntmp()
        P.add("dve", lambda e: e.tensor_tensor(out=tmp[t1][:, :], in0=ps[bq][:, :], in1=cst[csb][:, 0:T], op=ALU.mult),
              reads=[("ps", bq), ("cs", csb)], writes=[("tmp", t1)])
        t2 = ntmp()
        P.add("dve", lambda e: e.tensor_tensor(out=tmp[t2][:, :], in0=ps[bs][:, :], in1=cst[csb][:, T:2 * T], op=ALU.mult),
              reads=[("ps", bs), ("cs", csb)], writes=[("tmp", t2)])
        if dst is None:
            i = dkey[1]
            P.add("pool", lambda e: e.tensor_tensor(out=QT[0:64, i, 0, :], in0=tmp[t1][0:64, :], in1=tmp[t2][0:64, :], op=ALU.add),
                  reads=[("tmp", t1), ("tmp", t2)], writes=[dkey])
            P.add("dve", lambda e: e.tensor_tensor(out=QT[64:128, i, 1, :], in0=tmp[t1][64:128, :], in1=tmp[t2][64:128, :], op=ALU.add),
                  reads=[("tmp", t1), ("tmp", t2)], writes=[dkey])
            return
        P.add("pool", lambda e: e.tensor_tensor(out=dst, in0=tmp[t1][:, :], in1=tmp[t2][:, :], op=ALU.add),
              reads=[("tmp", t1), ("tmp", t2)], writes=[dkey])

    def attention(layer, j, after_pair=None):
        DEPTH = 2
        if layer == 0:
            lo, bmax, mt, mtoff = max(0, 4 * j - 1), 1, m0, 3
        else:
            lo, bmax, mt, mtoff = max(0, 4 * j - 16), 16, m1, 3
        kbs = [4 * j] + [kb for kb in range(lo, 4 * j + 4) if kb != 4 * j]
        P.phase = 'attn%d' % layer
        items = []
        for c in range(4):
            X, Y = acc(), acc()
            for idx, kb in enumerate(kbs):
                b0 = 4 * j - kb
                r_lo, r_hi = max(0, -b0), min(4, bmax + 1 - b0)
                if idx == 0:
                    r_lo, r_hi = 0, 4
                items.append(dict(c=c, X=X, Y=Y, idx=idx, kb=kb, c0=r_lo * 128, c1=r_hi * 128,
                                  mcol=(b0 + mtoff) * 128, last=(idx == len(kbs) - 1)))

        def stage1(it):
            c, kb, c0, c1 = it["c"], it["kb"], it["c0"], it["c1"]
            if layer == 0:
                slot = kb % 8
                kT = K0T[:, slot * 128:(slot + 1) * 128]
                kkey = ("k0", (kb // 4) % 2)
            else:
                slot = kb % 20
                kT = K1T[:, c, slot * 128:(slot + 1) * 128]
                kkey = ("k1", (kb // 4) % 5, c)
            st["gp"] = (st.get("gp", 0) + 2) % 4
            b0_ = st["gp"]
            for hb in range(2):
                mm(ps[b0_ + hb][:, c0:c1], kT, QT[:, c, hb, c0:c1], True, True, [kkey, ("qt", c)], [("ps", b0_ + hb)])
            pt = npt()
            it["pt"] = pt
            src = psall[:, b0_ * T:(b0_ + 2) * T].rearrange("p (b n) -> p b n", b=2)
            P.add("act", lambda e: e.activation(
                out=Pt[pt][:, :, c0:c1], in_=src[:, :, c0:c1], func=AF.Exp, bias=kbias[:, kb:kb + 1], scale=0.125),
                reads=[("ps", b0_), ("ps", b0_ + 1), "kbias"], writes=[("pt", pt, 0), ("pt", pt, 1)])
            mcol = it["mcol"]
            for hb in range(2):
                P.add("pool" if hb == 0 else "dve", lambda e, hb=hb: e.tensor_tensor(
                    out=Pt[pt][:, hb, c0:c1], in0=Pt[pt][:, hb, c0:c1], in1=mt[:, mcol + c0:mcol + c1], op=ALU.mult),
                    reads=[("pt", pt, hb), "m0", "m1"], writes=[("pt", pt, hb)])

        def stage2(it):
            c, kb, c0, c1, pt = it["c"], it["kb"], it["c0"], it["c1"], it["pt"]
            for hb in range(2):
                if layer == 0:
                    slot = kb % 8
                    vT = V0a[:, slot, 0:128] if hb == 0 else V0a[:, slot, 64:192]
                    vkey = ("v0", (kb // 4) % 2)
                else:
                    slot = kb % 20
                    vT = V1a[:, slot, c * 192:c * 192 + 128] if hb == 0 else V1a[:, slot, c * 192 + 64:c * 192 + 192]
                    vkey = ("v1", (kb // 4) % 5)
                O = it["X"] if hb == 0 else it["Y"]
                mm(ps[O][:, c0:c1], vT, Pt[pt][:, hb, c0:c1], it["idx"] == 0, it["last"], [vkey, ("pt", pt, hb)], [("ps", O)])
            if it["last"]:
                normalise(c, it["X"], it["Y"])
                if after_pair and c in after_pair:
                    after_pair[c]()
                    P.phase = 'attn%d' % layer

        def normalise(c, X, Y):
            for hb in range(2):
                O = X if hb == 0 else Y
                dl, dh = (64, 128) if hb == 0 else (0, 64)
                ol, oh = (0, 64) if hb == 0 else (64, 128)
                t1 = ntmp()
                if layer == 0:
                    col = c if hb == 0 else 4 + c
                    P.add("act", lambda e, O=O, t1=t1, dl=dl, dh=dh, col=col: e.activation(
                        out=tmp[t1][dl:dh, :], in_=ps[O][dl:dh, :], func=AF.Ln, bias=esink[dl:dh, col:col + 1]),
                        reads=[("ps", O), "esink"], writes=[("tmp", t1)])
                else:
                    P.add("act", lambda e, O=O, t1=t1, dl=dl, dh=dh: e.activation(
                        out=tmp[t1][dl:dh, :], in_=ps[O][dl:dh, :], func=AF.Ln),
                        reads=[("ps", O)], writes=[("tmp", t1)])
                P.add("act", lambda e, t1=t1, dl=dl, dh=dh: e.activation(
                    out=tmp[t1][dl:dh, :], in_=tmp[t1][dl:dh, :], func=AF.Exp, scale=-1.0),
                    reads=[("tmp", t1)], writes=[("tmp", t1)])
                P.add("dve", lambda e, O=O, t1=t1, dl=dl, dh=dh, ol=ol, oh=oh, c=c: e.tensor_tensor(
                    out=hid[ol:oh, c, :], in0=ps[O][ol:oh, :], in1=tmp[t1][dl:dh, :], op=ALU.mult),
                    reads=[("ps", O), ("tmp", t1)], writes=[("hid", c)])

        n = len(items)
        for i in range(n + DEPTH):
            if i < n:
                stage1(items[i])
            if i - DEPTH >= 0:
                stage2(items[i - DEPTH])

    def out_proj(layer):
        P.phase = 'outproj'
        nb = acc()
        for o in range(4):
            slot = unit_ahead(("out", layer, o))
            w = W3(slot)
            for oc in range(2):
                bk = gen()
                for kc in range(KC):
                    mm(ps[bk][:, :], w[:, kc, oc * 128:(oc + 1) * 128], hid[:, kc, :], kc == 0, kc == KC - 1,
                       [("w", slot), ("hid", kc)], [("ps", bk)])
                f = 2 * o + oc
                P.add("dve", lambda e, bk=bk, f=f: e.tensor_tensor(out=h[:, f, :], in0=ps[bk][:, :], in1=h[:, f, :], op=ALU.add),
                      reads=[("ps", bk), ("h", f)], writes=[("h", f)])
                sq_part1(f)
                if f > 0:
                    sq_part2(nb, f - 1, f - 1 == 0, False)
        sq_part2(nb, 7, False, True)
        return nb

    def ffn(l, nb_in, want_sums=True):
        rmsnorm(VC["g_ffn0"] if l == 0 else VC["g_ffn1"], nb_in)
        P.phase = 'ffn_gu'
        for i in range(11):
            sg_ = unit_ahead(("gate", l, i))
            su_ = unit_ahead(("up", l, i))
            for oc in range(2):
                bg = proj_fm(sg_, oc)
                bu = proj_fm(su_, oc)
                t1 = ntmp()
                sigmoid_to(t1, ps[bg][:, :], 1.0, [("ps", bg)])
                P.add("dve", lambda e, bg=bg, t1=t1: e.tensor_tensor(out=tmp[t1][:, :], in0=ps[bg][:, :], in1=tmp[t1][:, :], op=ALU.mult),
                      reads=[("ps", bg), ("tmp", t1)], writes=[("tmp", t1)])
                f = 2 * i + oc
                P.add("dve", lambda e, bu=bu, t1=t1, f=f: e.tensor_tensor(out=hid[:, f, :], in0=ps[bu][:, :], in1=tmp[t1][:, :], op=ALU.mult),
                      reads=[("ps", bu), ("tmp", t1)], writes=[("hid", f)])
        P.phase = 'ffn_dn'
        nb = acc() if want_sums else None
        for oc in range(8):
            s0_ = unit_ahead(("down", l, oc, 0))
            s1_ = unit_ahead(("down", l, oc, 1))
            bk = gen()
            for kc in range(FC):
                slot = s0_ if kc < 11 else s1_
                w = W3d(slot)
                mm(ps[bk][:, :], w[:, kc % 11, :], hid[:, kc, :], kc == 0, kc == FC - 1, [("w", slot), ("hid", kc)], [("ps", bk)])
            P.add("dve", lambda e, bk=bk, oc=oc: e.tensor_tensor(out=h[:, oc, :], in0=ps[bk][:, :], in1=h[:, oc, :], op=ALU.add),
                  reads=[("ps", bk), ("h", oc)], writes=[("h", oc)])
            if want_sums:
                sq_part1(oc)
                if oc > 0:
                    sq_part2(nb, oc - 1, oc - 1 == 0, False)
        if want_sums:
            sq_part2(nb, 7, False, True)
        return nb

    def layer0(j, nb_in, light=False):
        csb = j % 2
        rmsnorm(VC["g_ev"], nb_in)
        P.phase = 'inproj0'
        def conv_part1():
            P.phase = 'conv'
            for cc in range(4):
                sA = unit_ahead(("conv", cc, 0))
                sB = unit_ahead(("conv", cc, 1))
                bk = gen()
                for tap in range(31):
                    slot = sA if tap < 16 else sB
                    w = W3c(slot)
                    mm(ps[bk][:, :], w[:, tap % 16, :], cpad[:, cc, tap:tap + T], tap == 0, tap == 30, [("w", slot), ("cpad", cc)], [("ps", bk)])
                P.add("act", lambda e, bk=bk, cc=cc: e.activation(out=cu[:, cc, :], in_=ps[bk][:, :], func=AF.Identity,
                                                                  bias=vecs[:, VC["conv_b"] + cc:VC["conv_b"] + cc + 1]),
                      reads=[("ps", bk), "vecs"], writes=[("cu", cc)])
                P.add("dve", lambda e, cc=cc: e.tensor_copy(out=hid[:, 8 + cc, :], in_=cu[:, cc, :]),
                      reads=[("cu", cc)], writes=[("hid", 8 + cc)])
                P.add("act", lambda e, cc=cc: e.activation(out=hid[:, 12 + cc, :], in_=cu[:, cc, :], func=AF.Square),
                      reads=[("cu", cc)], writes=[("hid", 12 + cc)])
                P.add("pool", lambda e, cc=cc: e.tensor_copy(out=cpad[:, cc, 0:30], in_=cpad[:, cc, T:T + 30]),
                      reads=[("cpad", cc)], writes=[("cpad", cc)])

        def conv_part2():
            P.phase = 'convln'
            b1 = acc()
            for cc in range(4):
                mm(ps[b1][:, :], ones[:, :], hid[:, 8 + cc, :], cc == 0, cc == 3, [("hid", 8 + cc), "ones"], [("ps", b1)])
            b2 = acc()
            for cc in range(4):
                mm(ps[b2][:, :], ones[:, :], hid[:, 12 + cc, :], cc == 0, cc == 3, [("hid", 12 + cc), "ones"], [("ps", b2)])
            tm = ntmp()
            P.add("act", lambda e: e.activation(out=tmp[tm][:, :], in_=ps[b1][:, :], func=AF.Identity, scale=1.0 / 512),
                  reads=[("ps", b1)], writes=[("tmp", tm)])
            tq = ntmp()
            P.add("dve", lambda e: e.tensor_tensor(out=tmp[tq][:, :], in0=tmp[tm][:, :], in1=tmp[tm][:, :], op=ALU.mult),
                  reads=[("tmp", tm)], writes=[("tmp", tq)])
            P.add("dve", lambda e: e.scalar_tensor_tensor(out=tmp[tq][:, :], in0=ps[b2][:, :], scalar=1.0 / 512, in1=tmp[tq][:, :],
                                                          op0=ALU.mult, op1=ALU.subtract),
                  reads=[("ps", b2), ("tmp", tq)], writes=[("tmp", tq)])
            P.add("act", lambda e: e.activation(out=tmp[tq][:, :], in_=tmp[tq][:, :], func=AF.Ln, bias=1e-5),
                  reads=[("tmp", tq)], writes=[("tmp", tq)])
            P.add("act", lambda e: e.activation(out=tmp[tq][:, :], in_=tmp[tq][:, :], func=AF.Exp, scale=-0.5),
                  reads=[("tmp", tq)], writes=[("tmp", tq)])
            for cc in range(4):
                P.add("dve", lambda e, cc=cc: e.tensor_tensor(out=cu[:, cc, :], in0=cu[:, cc, :], in1=tmp[tm][:, :], op=ALU.subtract),
                      reads=[("cu", cc), ("tmp", tm)], writes=[("cu", cc)])
                P.add("dve", lambda e, cc=cc: e.tensor_tensor(out=cu[:, cc, :], in0=cu[:, cc, :], in1=tmp[tq][:, :], op=ALU.mult),
                      reads=[("cu", cc), ("tmp", tq)], writes=[("cu", cc)])
                P.add("act", lambda e, cc=cc: e.activation(out=cu[:, cc, :], in_=cu[:, cc, :], func=AF.Identity,
                                                           scale=vecs[:, VC["ln_g"] + cc:VC["ln_g"] + cc + 1],
                                                           bias=vecs[:, VC["ln_b"] + cc:VC["ln_b"] + cc + 1]),
                      reads=[("cu", cc), "vecs"], writes=[("cu", cc)])

        silu_tmp = {}

        def conv_silu_a(cc):
            ts_ = ntmp()
            silu_tmp[cc] = ts_
            sigmoid_to(ts_, cu[:, cc, :], 1.0, [("cu", cc)])

        def conv_silu_b(cc):
            ts_ = silu_tmp[cc]
            P.add("dve", lambda e: e.tensor_tensor(out=hid[:, 4 + cc, :], in0=cu[:, cc, :], in1=tmp[ts_][:, :], op=ALU.mult),
                  reads=[("cu", cc), ("tmp", ts_)], writes=[("hid", 4 + cc)])

        for i in range(4):
            slot = unit_ahead(("glu", i))
            ba = proj_fm(slot, 0)
            bb = proj_fm(slot, 1)
            t1 = ntmp()
            sigmoid_to(t1, ps[bb][:, :], 1.0, [("ps", bb)])
            P.add("dve", lambda e, ba=ba, t1=t1, i=i: e.tensor_tensor(out=cpad[:, i, 30:30 + T], in0=ps[ba][:, :], in1=tmp[t1][:, :], op=ALU.mult),
                  reads=[("ps", ba), ("tmp", t1)], writes=[("cpad", i)])
        slot = unit_ahead(("k0",))
        bq = proj_fm(slot, 0)
        bs = proj_fm(slot, 1)
        par = j % 2
        rotary_to(K0T[:, par * 512:(par + 1) * 512], ("k0", par), bq, bs, csb)
        slot = unit_ahead(("v0",))
        w = W3(slot)
        bk = gen()
        for tb in range(4):
            for kc in range(KC):
                mm(ps[bk][:, tb * 128:(tb + 1) * 128], xn[:, kc, tb * 128:(tb + 1) * 128], w[:, kc, 0:128], kc == 0, kc == KC - 1,
                   [("w", slot), ("xn", kc)], [("ps", bk)])
        src = ps[bk][:, :].rearrange("p (b x) -> p b x", x=128)
        P.add("act", lambda e: e.copy(out=V0a[:, par * 4:par * 4 + 4, 0:64], in_=src[:, :, 0:64]),
              reads=[("ps", bk)], writes=[("v0", par)])
        P.add("dve", lambda e: e.tensor_copy(out=V0a[:, par * 4:par * 4 + 4, 128:192], in_=src[:, :, 64:128]),
              reads=[("ps", bk)], writes=[("v0", par)])
        if light:
            for cc in range(4):
                P.add("pool", lambda e, cc=cc: e.tensor_copy(out=cpad[:, cc, 0:30], in_=cpad[:, cc, T:T + 30]),
                      reads=[("cpad", cc)], writes=[("cpad", cc)])
            return None
        conv_part1()
        conv_part2()
        for i in range(4):
            conv_silu_a(i)
            slot = unit_ahead(("q0", i))
            bq = proj_fm(slot, 0)
            bs = proj_fm(slot, 1)
            rotary_to(None, ("qt", i), bq, bs, csb)
            conv_silu_b(i)
        attention(0, j)
        nb = out_proj(0)
        return ffn(0, nb)

    def layer1(j, own, nb_in):
        csb = j % 2
        rmsnorm(VC["g_od"], nb_in)
        P.phase = 'inproj1'
        r5 = j % 5
        stages = []
        gpos = [0]

        def g_advance(n):
            ph = P.phase
            P.phase = 'gchain'
            for _ in range(n):
                if gpos[0] < len(stages):
                    stages[gpos[0]]()
                    gpos[0] += 1
            P.phase = ph

        if own:
            sg_ = [unit_ahead(("g1", i)) for i in range(2)]
            gbank = []
            for tb in range(4):
                bk = acc()
                gbank.append(bk)
                for ug in range(2):
                    w = W3(sg_[ug])
                    for kc in range(KC):
                        mm(ps[bk][:, ug * 256:(ug + 1) * 256], xn[:, kc, tb * 128:(tb + 1) * 128], w[:, kc, :], kc == 0, kc == KC - 1,
                           [("w", sg_[ug]), ("xn", kc)], [("ps", bk)])
            def gstage(eng, fn_of_tb, reads_of_tb, writes_of_tb):
                def run():
                    for tb in range(4):
                        P.add(eng, (lambda e, tb=tb: fn_of_tb(e, tb)), reads=reads_of_tb(tb), writes=writes_of_tb(tb))
                return run
            G = lambda tb: ps[gbank[tb]]
            gk = lambda tb: ("ps", gbank[tb])
            tk = lambda tb: ("gt", tb)
            col = lambda tb: 16 + 10 * tb
            stages[:] = [
                gstage("act", lambda e, tb: e.activation(out=gts[tb][:, :], in_=G(tb)[:, :], func=AF.Square), lambda tb: [gk(tb)], lambda tb: [tk(tb)]),
                gstage("dve", lambda e, tb: e.tensor_scalar(out=gts[tb][:, :], in0=gts[tb][:, :], scalar1=0.044715, scalar2=1.0, op0=ALU.mult, op1=ALU.add),
                       lambda tb: [tk(tb)], lambda tb: [tk(tb)]),
                gstage("dve", lambda e, tb: e.tensor_tensor(out=gts[tb][:, :], in0=G(tb)[:, :], in1=gts[tb][:, :], op=ALU.mult),
                       lambda tb: [gk(tb), tk(tb)], lambda tb: [tk(tb)]),
                gstage("act", lambda e, tb: e.activation(out=gts[tb][:, :], in_=gts[tb][:, :], func=AF.Exp, scale=-GK2), lambda tb: [tk(tb)], lambda tb: [tk(tb)]),
                gstage("act", lambda e, tb: e.activation(out=gts[tb][:, :], in_=gts[tb][:, :], func=AF.Ln, bias=1.0), lambda tb: [tk(tb)], lambda tb: [tk(tb)]),
                gstage("act", lambda e, tb: e.activation(out=gts[tb][:, :], in_=gts[tb][:, :], func=AF.Exp, scale=-1.0), lambda tb: [tk(tb)], lambda tb: [tk(tb)]),
                gstage("dve", lambda e, tb: e.tensor_tensor(out=G(tb)[:, :], in0=G(tb)[:, :], in1=gts[tb][:, :], op=ALU.mult),
                       lambda tb: [gk(tb), tk(tb)], lambda tb: [gk(tb)]),
                gstage("dve", lambda e, tb: e.bn_stats(out=small[:, col(tb):col(tb) + 6], in_=G(tb)[:, :]), lambda tb: [gk(tb)], lambda tb: ["smg%da" % tb]),
                gstage("dve", lambda e, tb: e.bn_aggr(out=small[:, col(tb) + 6:col(tb) + 8], in_=small[:, col(tb):col(tb) + 6]),
                       lambda tb: ["smg%da" % tb], lambda tb: ["smg%db" % tb]),
                gstage("act", lambda e, tb: e.activation(out=small[:, col(tb) + 8:col(tb) + 9], in_=small[:, col(tb) + 7:col(tb) + 8], func=AF.Ln, bias=1e-5),
                       lambda tb: ["smg%db" % tb], lambda tb: ["smg%dc" % tb]),
                gstage("act", lambda e, tb: e.activation(out=small[:, col(tb) + 9:col(tb) + 10], in_=small[:, col(tb) + 8:col(tb) + 9], func=AF.Exp, scale=-0.5),
                       lambda tb: ["smg%dc" % tb], lambda tb: ["smg%dd" % tb]),
                gstage("dve", lambda e, tb: e.tensor_scalar(out=G(tb)[:, :], in0=G(tb)[:, :], scalar1=small[:, col(tb) + 6:col(tb) + 7],
                                                            scalar2=small[:, col(tb) + 9:col(tb) + 10], op0=ALU.subtract, op1=ALU.mult),
                       lambda tb: [gk(tb), "smg%db" % tb, "smg%dd" % tb], lambda tb: [gk(tb)]),
                gstage("dve", lambda e, tb: e.tensor_tensor(out=gts[tb][:, :], in0=G(tb)[:, :], in1=rep[:, 0:512], op=ALU.mult),
                       lambda tb: [gk(tb), "rep"], lambda tb: [tk(tb)]),
                gstage("pool", lambda e, tb: e.tensor_tensor(out=gn[:, tb, :], in0=gts[tb][:, :], in1=rep[:, 512:1024], op=ALU.add),
                       lambda tb: [tk(tb), "rep"], lambda tb: [("gn", tb)]),
            ]
        for i in range(4):
            slot = unit_ahead(("k1", i))
            bq = proj_fm(slot, 0)
            bs = proj_fm(slot, 1)
            rotary_to(K1T[:, i, r5 * 512:(r5 + 1) * 512], ("k1", r5, i), bq, bs, csb)
            g_advance(2)
        sv = [unit_ahead(("v1", i)) for i in range(2)]
        for tb in range(4):
            bk = gen()
            for uv in range(2):
                w = W3(sv[uv])
                for kc in range(KC):
                    mm(ps[bk][:, uv * 256:(uv + 1) * 256], xn[:, kc, tb * 128:(tb + 1) * 128], w[:, kc, :], kc == 0, kc == KC - 1,
                       [("w", sv[uv]), ("xn", kc)], [("ps", bk)])
            slot = r5 * 4 + tb
            dst = V1a[:, slot, :].rearrange("p (c x) -> p c x", x=192)
            src = ps[bk][:, :].rearrange("p (c x) -> p c x", x=128)
            P.add("act", lambda e, dst=dst, src=src: e.copy(out=dst[:, :, 0:64], in_=src[:, :, 0:64]),
                  reads=[("ps", bk)], writes=[("v1", r5)])
            P.add("dve", lambda e, dst=dst, src=src: e.tensor_copy(out=dst[:, :, 128:192], in_=src[:, :, 64:128]),
                  reads=[("ps", bk)], writes=[("v1", r5)])
            g_advance(1)
        if not own:
            return
        for i in range(2):
            slot = unit_ahead(("u1", i))
            for oc in range(2):
                bk = proj_fm(slot, oc)
                f = 2 * i + oc
                P.add("act", lambda e, bk=bk, f=f: e.copy(out=cu[:, f, :], in_=ps[bk][:, :]),
                      reads=[("ps", bk)], writes=[("cu", f)])
                g_advance(1)
        uk = lambda f: ("cu", f)
        tk2 = lambda f: ("gt", f)
        ustages = [
            gstage("act", lambda e, f: e.activation(out=gts[f][:, :], in_=cu[:, f, :], func=AF.Square), lambda f: [uk(f)], lambda f: [tk2(f)]),
            gstage("dve", lambda e, f: e.tensor_scalar(out=gts[f][:, :], in0=gts[f][:, :], scalar1=0.044715, scalar2=1.0, op0=ALU.mult, op1=ALU.add),
                   lambda f: [tk2(f)], lambda f: [tk2(f)]),
            gstage("dve", lambda e, f: e.tensor_tensor(out=gts[f][:, :], in0=cu[:, f, :], in1=gts[f][:, :], op=ALU.mult),
                   lambda f: [uk(f), tk2(f)], lambda f: [tk2(f)]),
            gstage("act", lambda e, f: e.activation(out=gts[f][:, :], in_=gts[f][:, :], func=AF.Exp, scale=-GK2), lambda f: [tk2(f)], lambda f: [tk2(f)]),
            gstage("act", lambda e, f: e.activation(out=gts[f][:, :], in_=gts[f][:, :], func=AF.Ln, bias=1.0), lambda f: [tk2(f)], lambda f: [tk2(f)]),
            gstage("act", lambda e, f: e.activation(out=gts[f][:, :], in_=gts[f][:, :], func=AF.Exp, scale=-1.0), lambda f: [tk2(f)], lambda f: [tk2(f)]),
            gstage("pool", lambda e, f: e.tensor_tensor(out=cu[:, f, :], in0=cu[:, f, :], in1=gts[f][:, :], op=ALU.mult),
                   lambda f: [uk(f), tk2(f)], lambda f: [uk(f)]),
        ]
        stages.extend(ustages)
        for i in range(4):
            slot = unit_ahead(("q1", i))
            bq = proj_fm(slot, 0)
            bs = proj_fm(slot, 1)
            rotary_to(None, ("qt", i), bq, bs, csb)
            g_advance(2)
        g_advance(len(stages))
        if j + 1 < nch:
            x_dma(j + 1, 0)
            x_dma(j + 1, 1)
        attention(1, j)
        P.phase = 'gmlp'
        for c in range(4):
            bk = gen()
            for tb in range(4):
                for h2 in range(2):
                    g = 2 * c + h2
                    mm(ps[bk][h2 * 64:(h2 + 1) * 64, tb * 128:(tb + 1) * 128], gn[:, tb, g * 64:(g + 1) * 64], wst[:, g, :], True, True,
                       [("gn", tb), "wst"], [("ps", bk)])
            t1 = ntmp()
            for tb in range(4):
                P.add("dve", lambda e, bk=bk, t1=t1, tb=tb, c=c: e.tensor_tensor(
                    out=tmp[t1][:, tb * 128:(tb + 1) * 128], in0=ps[bk][:, tb * 128:(tb + 1) * 128], in1=sbrep[:, c, :], op=ALU.add),
                    reads=[("ps", bk), "sbrep"], writes=[("tmp", t1)])
            P.add("pool", lambda e, t1=t1, c=c: e.tensor_tensor(out=hid[:, 4 + c, :], in0=tmp[t1][:, :], in1=cu[:, c, :], op=ALU.mult),
                  reads=[("tmp", t1), ("cu", c)], writes=[("hid", 4 + c)])
        nb = out_proj(1)
        ffn(1, nb, want_sums=False)

    def final_out(j):
        P.phase = 'final'
        oj = j - nhalo
        for tb in range(4):
            banks = []
            for half in range(2):
                bk = gen()
                banks.append(bk)
                for f in range(4):
                    fc = half * 4 + f
                    P.add("pe", lambda e, bk=bk, f=f, fc=fc, tb=tb: e.transpose(
                        ps[bk][:, f * 128:(f + 1) * 128], h[:, fc, tb * 128:(tb + 1) * 128], ident[:, :]),
                        reads=[("h", fc), "ident"], writes=[("ps", bk)])
                tj = ntmp()
                P.add("act", lambda e, bk=bk, tj=tj, half=half: e.activation(out=tmp[tj][:, :], in_=ps[bk][:, :], func=AF.Square,
                                                                             accum_out=small[:, 56 + half:57 + half]),
                      reads=[("ps", bk)], writes=[("tmp", tj), "sf%d" % half])
            P.add("dve", lambda e: e.tensor_tensor(out=small[:, 58:59], in0=small[:, 56:57], in1=small[:, 57:58], op=ALU.add),
                  reads=["sf0", "sf1"], writes=["sf2"])
            P.add("act", lambda e: e.activation(out=small[:, 59:60], in_=small[:, 58:59], func=AF.Ln, bias=1e-6, scale=1.0 / D),
                  reads=["sf2"], writes=["sf3"])
            P.add("act", lambda e: e.activation(out=small[:, 60:61], in_=small[:, 59:60], func=AF.Exp, scale=-0.5),
                  reads=["sf3"], writes=["sf4"])
            r0 = oj * T + tb * 128
            for half in range(2):
                bk = banks[half]
                ts_ = ntmp()
                P.add("dve", lambda e, bk=bk, half=half, ts_=ts_: e.scalar_tensor_tensor(
                    out=tmp[ts_][:, :], in0=ps[bk][:, :], scalar=small[:, 60:61],
                    in1=rep[:, 1024 + half * 512:1024 + (half + 1) * 512], op0=ALU.mult, op1=ALU.mult),
                    reads=[("ps", bk), "sf4", "rep"], writes=[("tmp", ts_)])
                P.add("pool", lambda e, r0=r0, half=half, ts_=ts_: e.dma_start(
                    out=out_d[r0:r0 + 128, half * 512:(half + 1) * 512], in_=tmp[ts_][:, :]),
                    reads=[("tmp", ts_)], dma="o%d" % ts_)

    cast_upto(CAST_AHEAD)
    load_cs(0)
    nb = load_x(0)
    for j in range(nch):
        own = j >= nhalo
        light = (j == 0 and nhalo >= 5)
        if j + 1 < nch:
            load_cs(j + 1)
        nb = layer0(j, nb, light)
        if j + 1 < nch and not own:
            x_dma(j + 1, 0)
            x_dma(j + 1, 1)
        if not light:
            layer1(j, own, nb)
        if own:
            final_out(j)
        if j + 1 < nch:
            nb = load_x(j + 1, prefetched=True)
    cast_upto(len(cast_order))
    P.emit()
    build_program.last_prog = P
    return nc


def _run(inputs, B, S, nhalo, runner=None):
    inp = {k: np.asarray(v, dtype=np.float32) for k, v in inputs.items()}
    half = S // 2
    nown = half // T
    nch = nown + nhalo
    wall = _build_wall(inp)
    consts = _build_consts(inp)
    x = inp["x"]
    in_maps = []
    for core in range(2 * B):
        b, hf = core // 2, core % 2
        s0 = hf * half
        xe = np.zeros((nch * T, D), np.float32)
        lo = s0 - nhalo * T
        src_lo = max(lo, 0)
        xe[src_lo - lo:, :] = x[b, src_lo:s0 + half, :]
        cs, kb = _core_tables(s0, nhalo, nch)
        m = dict(x=xe, wall=wall, cs=cs, kbias=kb)
        m.update(consts)
        in_maps.append(m)
    nc = build_program(nown, nhalo)
    if runner is None:
        res = run_bass_kernel_spmd(nc, in_maps, core_ids=list(range(2 * B)))
        results = res.results
    else:
        results = runner(nc, in_maps)
    out = np.zeros((B, S, D), np.float32)
    for core in range(2 * B):
        b, hf = core // 2, core % 2
        out[b, hf * half:(hf + 1) * half, :] = results[core]["out"]
    return out


def kernel(**inputs):
    return _run(inputs, B=4, S=8192, nhalo=5)
```

```python
import numpy as np
import concourse.bass as bass
import concourse.mybir as mybir
from concourse.bass_utils import run_bass_kernel_spmd

F32 = mybir.dt.float32
BF16 = mybir.dt.bfloat16
AF = mybir.ActivationFunctionType
ALU = mybir.AluOpType

T = 512
D = 1024
KC = 8
DFF = 2816
FC = 22
NEG = -30000.0
NSLOT = 6
CAST_AHEAD = 16


class Prog:
    def __init__(self, nc):
        self.nc = nc
        self.ops = []
        self.state = {}
        self.last_dma = {}
        self.phase = ''
        self.names = {}

    def add(self, eng, fn, reads=(), writes=(), dma=None):
        idx = len(self.ops)
        sem = dma if dma else eng
        deps = {}
        reads = list(reads)
        writes = list(writes)
        writes += [k for k in reads if isinstance(k, tuple) and k[0] == "ps"]
        reads = [k for k in reads if not (isinstance(k, tuple) and k[0] == "ps")]

        def need(d):
            for s, i in d.items():
                if deps.get(s, -1) < i:
                    deps[s] = i

        for k in reads:
            st = self.state.setdefault(k, {"w": {}, "r": {}})
            need(st["w"])
        for k in writes:
            st = self.state.setdefault(k, {"w": {}, "r": {}})
            need(st["w"])
            need(st["r"])
        for k in reads:
            self.state[k]["r"][sem] = idx
        for k in writes:
            st = self.state[k]
            st["w"] = {sem: idx}
            st["r"] = {}
        if eng == "pe" and not dma:
            deps.pop("pe", None)
        if dma:
            prev = self.last_dma.get(sem)
            if prev is not None and deps.get(sem, -1) < prev:
                deps[sem] = prev
            self.last_dma[sem] = idx
        self.ops.append(dict(eng=eng, fn=fn, sem=sem, deps=deps, dma=bool(dma), signal=False, tok=None, phase=self.phase))
        return idx

    def emit(self):
        nc = self.nc
        ops = self.ops
        for op in ops:
            for s, i in op["deps"].items():
                ops[i]["signal"] = True
        cnt = {}
        for op in ops:
            s = op["sem"]
            if op["dma"]:
                cnt[s] = cnt.get(s, 0) + 16
                op["tok"] = cnt[s]
            elif op["signal"]:
                cnt[s] = cnt.get(s, 0) + 1
                op["tok"] = cnt[s]
        sems = {s: nc.alloc_semaphore("sem_" + str(s)) for s in sorted(set(op["sem"] for op in ops))}

        def run_engine(ename):
            def body(eng):
                known = {}
                for op in ops:
                    if op["eng"] != ename:
                        continue
                    for s, i in op["deps"].items():
                        v = ops[i]["tok"]
                        if known.get(s, 0) < v:
                            eng.wait_ge(sems[s], v)
                            known[s] = v
                    ins = op["fn"](eng)
                    try:
                        self.names[ins.ins.name] = (op["phase"], ename)
                    except Exception:
                        pass
                    if op["dma"]:
                        ins.then_inc(sems[op["sem"]], 16)
                    elif op["signal"]:
                        ins.then_inc(sems[op["sem"]], 1)
                for s in sorted(set(op["sem"] for op in ops if op["eng"] == ename and op["dma"])):
                    if known.get(s, 0) < cnt[s]:
                        eng.wait_ge(sems[s], cnt[s])
                        known[s] = cnt[s]
            return body

        with nc.Block() as block:
            block.sync(run_engine("sp"))
            block.tensor(run_engine("pe"))
            block.scalar(run_engine("act"))
            block.vector(run_engine("dve"))
            block.gpsimd(run_engine("pool"))


def _swap_cols(base):
    return list(range(base + 8, base + 16)) + list(range(base, base + 8)) + list(range(base + 16, base + 64))


def _unit_table():
    names = []
    for i in range(4):
        names.append(("q0", i))
    names.append(("k0",))
    for i in range(4):
        names.append(("glu", i))
    names.append(("v0",))
    for cc in range(4):
        names.append(("conv", cc, 0))
        names.append(("conv", cc, 1))
    for o in range(4):
        names.append(("out", 0, o))
    for l in range(2):
        for i in range(11):
            names.append(("gate", l, i))
            names.append(("up", l, i))
        for oc in range(8):
            names.append(("down", l, oc, 0))
            names.append(("down", l, oc, 1))
    for i in range(4):
        names.append(("k1", i))
    for i in range(2):
        names.append(("v1", i))
    for i in range(4):
        names.append(("q1", i))
    for i in range(2):
        names.append(("u1", i))
    for i in range(2):
        names.append(("g1", i))
    for o in range(4):
        names.append(("out", 1, o))
    return names


UNIT_NAMES = _unit_table()
UNIT_IDX = {n: i for i, n in enumerate(UNIT_NAMES)}
NU = len(UNIT_NAMES)


def _std_unit(W, cols, rows=None):
    cols = list(cols)
    if len(cols) < 256:
        cols = cols + [cols[0]] * (256 - len(cols))
    Wc = W[:, cols]
    if rows is not None:
        Wc = Wc[rows, :]
    return np.ascontiguousarray(Wc.reshape(8, 128, 256).transpose(1, 0, 2).reshape(128, 2048))


def _build_wall(inp):
    wall = np.zeros((NU, 128, 2048), np.float32)
    w_in0 = inp["ev_w_in"][0]
    w_out0 = inp["ev_w_out"][0]
    w_in1 = inp["od_w_in"][0]
    w_out1 = inp["od_w_out"][0]
    conv_w = inp["ev_conv_w"][0]
    for i in range(4):
        ha, hb = i * 64, (4 + i) * 64
        cols = list(range(ha, ha + 64)) + list(range(hb, hb + 64)) + _swap_cols(ha) + _swap_cols(hb)
        wall[UNIT_IDX[("q0", i)]] = _std_unit(w_in0, cols)
    cols = list(range(512, 640)) + _swap_cols(512) + _swap_cols(576)
    wall[UNIT_IDX[("k0",)]] = _std_unit(w_in0, cols)
    for i in range(4):
        cols = list(range(768 + 128 * i, 768 + 128 * i + 128)) + list(range(1280 + 128 * i, 1280 + 128 * i + 128))
        wall[UNIT_IDX[("glu", i)]] = _std_unit(w_in0, cols)
    wall[UNIT_IDX[("v0",)]] = _std_unit(w_in0, list(range(640, 768)))
    ar = np.arange(128)
    for cc in range(4):
        for half in range(2):
            u = np.zeros((128, 16, 128), np.float32)
            taps = range(16) if half == 0 else range(16, 31)
            for jj, tap in enumerate(taps):
                u[ar, jj, ar] = conv_w[tap, cc * 128:(cc + 1) * 128]
            wall[UNIT_IDX[("conv", cc, half)]] = u.reshape(128, 2048)
    rows0 = []
    for c in range(4):
        rows0 += list(range(c * 64, c * 64 + 64)) + list(range((4 + c) * 64, (4 + c) * 64 + 64))
    rows0 += list(range(512, 1024))
    for o in range(4):
        wall[UNIT_IDX[("out", 0, o)]] = _std_unit(w_out0, range(256 * o, 256 * o + 256), rows=rows0)
        wall[UNIT_IDX[("out", 1, o)]] = _std_unit(w_out1, range(256 * o, 256 * o + 256))
    for l in range(2):
        wg, wu, wd = inp["ffn_w_gate"][l], inp["ffn_w_up"][l], inp["ffn_w_down"][l]
        for i in range(11):
            wall[UNIT_IDX[("gate", l, i)]] = _std_unit(wg, range(256 * i, 256 * i + 256))
            wall[UNIT_IDX[("up", l, i)]] = _std_unit(wu, range(256 * i, 256 * i + 256))
        for oc in range(8):
            for half in range(2):
                blk = wd[half * 1408:(half + 1) * 1408, oc * 128:(oc + 1) * 128]
                u = blk.reshape(11, 128, 128).transpose(1, 0, 2).reshape(128, 1408)
                wall[UNIT_IDX[("down", l, oc, half)], :, :1408] = u
    for i in range(4):
        ha, hb = 128 * i, 128 * i + 64
        cols = list(range(ha, ha + 128)) + _swap_cols(ha) + _swap_cols(hb)
        wall[UNIT_IDX[("q1", i)]] = _std_unit(w_in1, cols)
        cols = list(range(512 + ha, 512 + ha + 128)) + _swap_cols(512 + ha) + _swap_cols(512 + hb)
        wall[UNIT_IDX[("k1", i)]] = _std_unit(w_in1, cols)
    for i in range(2):
        wall[UNIT_IDX[("v1", i)]] = _std_unit(w_in1, range(1024 + 256 * i, 1024 + 256 * i + 256))
        wall[UNIT_IDX[("u1", i)]] = _std_unit(w_in1, range(1536 + 256 * i, 1536 + 256 * i + 256))
        wall[UNIT_IDX[("g1", i)]] = _std_unit(w_in1, range(2048 + 256 * i, 2048 + 256 * i + 256))
    return wall


def _fm(v):
    return np.ascontiguousarray(v.reshape(-1, 128).T)


VC = dict(g_ev=0, g_ffn0=8, g_od=16, g_ffn1=24, conv_b=32, ln_g=36, ln_b=40, sink=44)
NV = 52


def _build_consts(inp):
    vecs = np.zeros((128, NV), np.float32)
    vecs[:, 0:8] = _fm(inp["ev_norm_g"][0])
    vecs[:, 8:16] = _fm(inp["ffn_norm_g"][0])
    vecs[:, 16:24] = _fm(inp["od_norm_g"][0])
    vecs[:, 24:32] = _fm(inp["ffn_norm_g"][1])
    vecs[:, 32:36] = _fm(inp["ev_conv_b"][0])
    vecs[:, 36:40] = _fm(inp["ev_conv_ln_g"][0])
    vecs[:, 40:44] = _fm(inp["ev_conv_ln_b"][0])
    sk = inp["ev_sinks"][0]
    for c in range(4):
        vecs[64:128, 44 + c] = sk[c]
        vecs[0:64, 48 + c] = sk[4 + c]
    rep = np.zeros((128, 2048), np.float32)
    rep[:, 0:512] = inp["od_sgu_ln_g"][0][None, :]
    rep[:, 512:1024] = inp["od_sgu_ln_b"][0][None, :]
    rep[:, 1024:2048] = inp["final_norm_g"][None, :]
    sb = inp["od_spatial_b"][0]
    sbrep = np.zeros((128, 4, 128), np.float32)
    for c in range(4):
        sbrep[0:64, c, :] = sb[2 * c][None, :]
        sbrep[64:128, c, :] = sb[2 * c + 1][None, :]
    ws = inp["od_spatial_w"][0]
    tri = (np.arange(128)[:, None] <= np.arange(128)[None, :])
    wst = np.zeros((128, 8, 128), np.float32)
    for g in range(8):
        wst[:, g, :] = np.where(tri, ws[g].T, 0.0)
    kj = np.arange(128)[:, None]
    qi = np.arange(128)[None, :]
    m0 = np.zeros((128, 8, 128), np.float32)
    for bi, b in enumerate(range(-3, 5)):
        dl = 128 * b + qi - kj
        m0[:, bi, :] = ((dl >= 0) & (dl <= 127)).astype(np.float32)
    m1 = np.zeros((128, 23, 128), np.float32)
    for bi, b in enumerate(range(-3, 20)):
        dl = 128 * b + qi - kj
        m1[:, bi, :] = (((dl >= 0) & (dl <= 128)).astype(np.float32)
                        + ((dl >= 0) & (dl <= 512) & (dl % 4 == 0)).astype(np.float32)
                        + ((dl >= 0) & (dl <= 2048) & (dl % 16 == 0)).astype(np.float32))
    return dict(vecs=vecs, rep=rep, sbrep=sbrep.reshape(128, 512), wst=wst.reshape(128, 1024),
                m0=m0.reshape(128, 1024), m1=m1.reshape(128, 23 * 128),
                ident=np.eye(128, dtype=np.float32))


def _core_tables(s0, nhalo, nch):
    ntok = nch * T
    pos = s0 - nhalo * T + np.arange(ntok)
    posc = np.maximum(pos, 0).astype(np.float32)
    inv = (np.float32(500000.0) ** (-np.arange(8, dtype=np.float32) * np.float32(2.0 / 16))).astype(np.float32)
    ang = (posc[:, None] * inv[None, :]).astype(np.float32)
    cos = np.cos(ang).astype(np.float32)
    sin = np.sin(ang).astype(np.float32)
    cs = np.zeros((nch, 128, 2 * T), np.float32)
    cs[:, :, 0:T] = 1.0
    for p in range(128):
        i = p % 64
        if i < 16:
            f = i % 8
            cs[:, p, 0:T] = cos[:, f].reshape(nch, T)
            sg = -1.0 if i < 8 else 1.0
            cs[:, p, T:2 * T] = sg * sin[:, f].reshape(nch, T)
    kb = np.zeros((128, nch * 4), np.float32)
    for b in range(nch * 4):
        if s0 - nhalo * T + b * 128 < 0:
            kb[:, b] = NEG
    return cs, kb


def build_program(nown, nhalo):
    nch = nown + nhalo
    ntok = nch * T
    nc = bass.Bass("TRN2", target_bir_lowering=False)
    dI = lambda name, shape, dt=F32: nc.dram_tensor(name, list(shape), dt, kind="ExternalInput").ap()
    x_d = dI("x", [ntok, D])
    wall_d = dI("wall", [NU, 128, 2048])
    vecs_d = dI("vecs", [128, NV])
    rep_d = dI("rep", [128, 2048])
    sbrep_d = dI("sbrep", [128, 512])
    wst_d = dI("wst", [128, 1024])
    m0_d = dI("m0", [128, 1024])
    m1_d = dI("m1", [128, 23 * 128])
    ident_d = dI("ident", [128, 128])
    cs_d = dI("cs", [nch, 128, 2 * T])
    kb_d = dI("kbias", [128, nch * 4])
    out_d = nc.dram_tensor("out", [nown * T, D], F32, kind="ExternalOutput").ap()
    wsc_d = nc.dram_tensor("wsc", [NU, 128, 2048], BF16, kind="Internal").ap()

    SB = lambda name, shape, dt: nc.alloc_sbuf_tensor("sb_" + name, list(shape), dt)
    h = SB("h", [128, KC, T], F32)
    xn = SB("xn", [128, KC, T], BF16)
    hid = SB("hid", [128, FC, T], BF16)
    xin = [SB("xin%d" % i, [128, D], F32) for i in range(2)]
    gts = [xin[0][:, 0:512], xin[0][:, 512:1024], xin[1][:, 0:512], xin[1][:, 512:1024]]
    cst = [SB("cst%d" % i, [128, 2 * T], F32) for i in range(2)]
    QT = SB("QT", [128, 4, 2, T], BF16)
    K0T = SB("K0T", [128, 8 * 128], BF16)
    V0a = SB("V0a", [128, 8, 192], BF16)
    K1T = SB("K1T", [128, 4, 20 * 128], BF16)
    V1a = SB("V1a", [128, 20, 768], BF16)
    cpad = SB("cpad", [128, 4, 30 + T], BF16)
    cu = SB("cu", [128, 4, T], F32)
    gn = SB("gn", [128, 4, 512], BF16)
    wst = SB("wst", [128, 8, 128], BF16)
    sbrep = SB("sbrep", [128, 4, 128], F32)
    rep = SB("rep", [128, 2048], F32)
    m0 = SB("m0", [128, 1024], BF16)
    m1 = SB("m1", [128, 23 * 128], BF16)
    NPT = 3
    Pt = [SB("Pt%d" % i, [128, 2, T], BF16) for i in range(NPT)]
    NTMP = 6
    tmp = [SB("tmp%d" % i, [128, T], F32) for i in range(NTMP)]
    wring = SB("wring", [128, NSLOT, 2048], BF16)
    vecs = SB("vecs", [128, NV], F32)
    esink = SB("esink", [128, 8], F32)
    kbias = SB("kbias", [128, nch * 4], F32)
    ident = SB("ident", [128, 128], F32)
    ones = SB("ones", [128, 128], BF16)
    small = SB("small", [128, 64], F32)
    psall = nc.alloc_psum_tensor("psall", [128, 8 * T], F32)
    ps = [psall[:, i * T:(i + 1) * T] for i in range(8)]

    P = Prog(nc)
    st = dict(gen=0, acc=0, tmp=0, pt=0, wcnt=0, cast=0)

    def gen():
        st["gen"] = (st["gen"] + 1) % 4
        return st["gen"]

    def acc():
        st["acc"] = (st["acc"] + 1) % 4
        return 4 + st["acc"]

    def ntmp():
        st["tmp"] = (st["tmp"] + 1) % NTMP
        return st["tmp"]

    def npt():
        st["pt"] = (st["pt"] + 1) % NPT
        return st["pt"]

    cast_order = []
    seen = set()

    def cast_upto(n):
        while st["cast"] < min(n, len(cast_order)):
            u = cast_order[st["cast"]]
            P.add("pool", lambda e, u=u: e.dma_start(out=wsc_d[u], in_=wall_d[u], max_dma_last_dim=4096),
                  writes=[("wsc", u)], dma="cast%d" % (st["cast"] % 8))
            st["cast"] += 1

    def preregister(names):
        for n in names:
            u = UNIT_IDX[n]
            if u not in seen:
                seen.add(u)
                cast_order.append(u)

    l0_names = [("q0", i) for i in range(4)] + [("k0",)] + [("glu", i) for i in range(4)] + [("v0",)]
    l0_names += [("conv", cc, hf) for cc in range(4) for hf in range(2)] + [("out", 0, o) for o in range(4)]

    def ffn_names(l):
        r = []
        for i in range(11):
            r += [("gate", l, i), ("up", l, i)]
        for oc in range(8):
            r += [("down", l, oc, 0), ("down", l, oc, 1)]
        return r

    l1_halo = [("k1", i) for i in range(4)] + [("v1", i) for i in range(2)]
    l1_own = [("q1", i) for i in range(4)] + [("u1", i) for i in range(2)] + [("g1", i) for i in range(2)]
    l1_own += [("out", 1, o) for o in range(4)]
    preregister(l0_names + ffn_names(0) + l1_halo + l1_own + ffn_names(1))

    def unit_ahead(name):
        ncols = 1408 if name[0] == "down" else (1920 if (name[0] == "conv" and name[2] == 1) else 2048)
        u = UNIT_IDX[name]
        pos = cast_order.index(u)
        cast_upto(pos + 1 + CAST_AHEAD)
        slot = st["wcnt"] % NSLOT
        st["wcnt"] += 1
        P.add("sp", lambda e, u=u, slot=slot: e.dma_start(out=wring[:, slot, 0:ncols], in_=wsc_d[u][:, 0:ncols]),
              reads=[("wsc", u)], writes=[("w", slot)], dma="w%d" % slot)
        return slot

    W3 = lambda slot: wring[:, slot, :].rearrange("p (k n) -> p k n", n=256)
    W3d = lambda slot: wring[:, slot, 0:1408].rearrange("p (k n) -> p k n", n=128)
    W3c = lambda slot: wring[:, slot, :].rearrange("p (k n) -> p k n", n=128)

    def ld(eng, dst, src, key, sem):
        if eng == "pool":
            P.add(eng, lambda e: e.dma_start(out=dst, in_=src, max_dma_last_dim=4096), writes=[key], dma=sem)
        else:
            P.add(eng, lambda e: e.dma_start(out=dst, in_=src), writes=[key], dma=sem)

    ld("sp", vecs[:, :], vecs_d[:, :], "vecs", "s0")
    ld("sp", ident[:, :], ident_d[:, :], "ident", "s1")
    ld("sp", rep[:, :], rep_d[:, :], "rep", "s2")
    ld("sp", sbrep[:, :, :], sbrep_d.rearrange("p (c t) -> p c t", t=128), "sbrep", "s3")
    ld("sp", kbias[:, :], kb_d[:, :], "kbias", "s0")
    ld("pool", wst[:, :, :], wst_d.rearrange("p (g t) -> p g t", t=128), "wst", "c0")
    ld("pool", m0[:, :], m0_d[:, :], "m0", "c1")
    ld("pool", m1[:, :], m1_d[:, :], "m1", "c2")
    P.add("dve", lambda e: e.memset(ones[:, :], 1.0), writes=["ones"])
    P.add("dve", lambda e: e.memset(QT[:, :, :, :], 0.0), writes=[("qt", i) for i in range(4)])
    P.add("dve", lambda e: e.memset(K0T[:, :], 0.0), writes=[("k0", 0), ("k0", 1)])
    P.add("dve", lambda e: e.memset(V0a[:, :, :], 0.0), writes=[("v0", 0), ("v0", 1)])
    P.add("dve", lambda e: e.memset(V0a[:, :, 64:128], 1.0), writes=[("v0", 0), ("v0", 1)])
    P.add("dve", lambda e: e.memset(V1a[:, :, :], 0.0), writes=[("v1", i) for i in range(5)])
    v1v = V1a[:, :, :].rearrange("p s (c x) -> p s c x", x=192)
    for c in range(4):
        P.add("dve", lambda e, c=c: e.memset(v1v[:, :, c, 64:128], 1.0), writes=[("v1", i) for i in range(5)])
    P.add("dve", lambda e: e.memset(cpad[:, :, :], 0.0), writes=[("cpad", i) for i in range(4)])
    P.add("act", lambda e: e.activation(out=esink[:, :], in_=vecs[:, VC["sink"]:VC["sink"] + 8], func=AF.Exp),
          reads=["vecs"], writes=["esink"])

    def mm(out, lhsT, rhs, start, stop, reads, writes):
        P.add("pe", lambda e: e.matmul(out, lhsT=lhsT, rhs=rhs, start=start, stop=stop), reads=reads, writes=writes)

    def x_dma(j, b):
        gb = j * 4 + b
        buf = gb % 2
        P.add("sp", lambda e: e.dma_start(out=xin[buf][:, :], in_=x_d[gb * 128:(gb + 1) * 128, :]),
              writes=[("xin", buf), ("gt", 2 * buf), ("gt", 2 * buf + 1)], dma="x%d" % buf)

    def load_x(j, prefetched=False):
        P.phase = 'xT'
        for b in range(4):
            gb = j * 4 + b
            buf = gb % 2
            if not (prefetched and b < 2):
                x_dma(j, b)
            for half in range(2):
                bk = gen()
                for f in range(4):
                    fc = half * 4 + f
                    P.add("pe", lambda e, bk=bk, f=f, fc=fc, buf=buf: e.transpose(
                        ps[bk][:, f * 128:(f + 1) * 128], xin[buf][:, fc * 128:(fc + 1) * 128], ident[:, :]),
                        reads=[("xin", buf), ("gt", 2 * buf), ("gt", 2 * buf + 1), "ident"], writes=[("ps", bk)])
                eng = "act" if half == 0 else "dve"
                dst = h[:, half * 4:half * 4 + 4, b * 128:(b + 1) * 128]
                src = ps[bk][:, :].rearrange("p (f t) -> p f t", t=128)
                if eng == "act":
                    P.add("act", lambda e, dst=dst, src=src: e.copy(out=dst, in_=src),
                          reads=[("ps", bk)], writes=[("h", half * 4 + f) for f in range(4)])
                else:
                    P.add("dve", lambda e, dst=dst, src=src: e.tensor_copy(out=dst, in_=src),
                          reads=[("ps", bk)], writes=[("h", half * 4 + f) for f in range(4)])
        nb = acc()
        for fc in range(KC):
            sq_acc(nb, fc, fc == 0, fc == KC - 1)
        return nb

    def load_cs(j):
        buf = j % 2
        P.add("pool", lambda e: e.dma_start(out=cst[buf][:, :], in_=cs_d[j]), writes=[("cs", buf)], dma="cs%d" % buf)

    def rstd_from_bank(bk, scale, eps):
        t1 = ntmp()
        P.add("act", lambda e: e.activation(out=tmp[t1][:, :], in_=ps[bk][:, :], func=AF.Ln, bias=eps, scale=scale),
              reads=[("ps", bk)], writes=[("tmp", t1)])
        t2 = ntmp()
        P.add("act", lambda e: e.activation(out=tmp[t2][:, :], in_=tmp[t1][:, :], func=AF.Exp, scale=-0.5),
              reads=[("tmp", t1)], writes=[("tmp", t2)])
        return t2

    def sigmoid_to(t, src, scale, reads):
        P.add("act", lambda e: e.activation(out=tmp[t][:, :], in_=src, func=AF.Exp, scale=-scale),
              reads=reads, writes=[("tmp", t)])
        P.add("act", lambda e: e.activation(out=tmp[t][:, :], in_=tmp[t][:, :], func=AF.Ln, bias=1.0),
              reads=[("tmp", t)], writes=[("tmp", t)])
        P.add("act", lambda e: e.activation(out=tmp[t][:, :], in_=tmp[t][:, :], func=AF.Exp, scale=-1.0),
              reads=[("tmp", t)], writes=[("tmp", t)])

    GK2 = 2.0 * 0.7978845608028654

    def gelu_sig(bk):
        t = ntmp()
        P.add("act", lambda e: e.activation(out=tmp[t][:, :], in_=ps[bk][:, :], func=AF.Square),
              reads=[("ps", bk)], writes=[("tmp", t)])
        P.add("dve", lambda e: e.tensor_scalar(out=tmp[t][:, :], in0=tmp[t][:, :], scalar1=0.044715, scalar2=1.0,
                                               op0=ALU.mult, op1=ALU.add),
              reads=[("tmp", t)], writes=[("tmp", t)])
        P.add("dve", lambda e: e.tensor_tensor(out=tmp[t][:, :], in0=ps[bk][:, :], in1=tmp[t][:, :], op=ALU.mult),
              reads=[("ps", bk), ("tmp", t)], writes=[("tmp", t)])
        sigmoid_to(t, tmp[t][:, :], GK2, [("tmp", t)])
        return t

    def sq_part1(fc):
        P.add("act", lambda e: e.activation(out=xn[:, fc, :], in_=h[:, fc, :], func=AF.Square),
              reads=[("h", fc)], writes=[("xn", fc)])

    def sq_part2(nb, fc, first, last):
        mm(ps[nb][:, :], ones[:, :], xn[:, fc, :], first, last, [("xn", fc), "ones"], [("ps", nb)])

    def sq_acc(nb, fc, first, last):
        sq_part1(fc)
        sq_part2(nb, fc, first, last)

    def rmsnorm(gcol, nb):
        P.phase = 'norm'
        t1 = ntmp()
        P.add("act", lambda e: e.activation(out=tmp[t1][:, :], in_=ps[nb][:, :], func=AF.Ln, bias=1e-6, scale=1.0 / D),
              reads=[("ps", nb)], writes=[("tmp", t1)])
        P.add("act", lambda e: e.activation(out=ps[nb][:, :], in_=tmp[t1][:, :], func=AF.Exp, scale=-0.5),
              reads=[("tmp", t1)], writes=[("ps", nb)])
        for fc in range(KC):
            P.add("dve", lambda e, fc=fc: e.scalar_tensor_tensor(
                out=xn[:, fc, :], in0=h[:, fc, :], scalar=vecs[:, gcol + fc:gcol + fc + 1], in1=ps[nb][:, :],
                op0=ALU.mult, op1=ALU.mult),
                reads=[("h", fc), ("ps", nb), "vecs"], writes=[("xn", fc)])

    def proj_fm(slot, half):
        bk = gen()
        w = W3(slot)
        for kc in range(KC):
            mm(ps[bk][:, :], w[:, kc, half * 128:(half + 1) * 128], xn[:, kc, :], kc == 0, kc == KC - 1,
               [("w", slot), ("xn", kc)], [("ps", bk)])
        return bk

    def rotary_to(dst, dkey, bq, bs, csb):
        t1 = ntmp()
        P.add("dve", lambda e: e.tensor_tensor(out=tmp[t1][:, :], in0=ps[bq][:, :], in1=cst[csb][:, 0:T], op=ALU.mult),
              reads=[("ps", bq), ("cs", csb)], writes=[("tmp", t1)])
        t2 = ntmp()
        P.add("dve", lambda e: e.tensor_tensor(out=tmp[t2][:, :], in0=ps[bs][:, :], in1=cst[csb][:, T:2 * T], op=ALU.mult),
              reads=[("ps", bs), ("cs", csb)], writes=[("tmp", t2)])
        if dst is None:
            i = dkey[1]
            P.add("pool", lambda e: e.tensor_tensor(out=QT[0:64, i, 0, :], in0=tmp[t1][0:64, :], in1=tmp[t2][0:64, :], op=ALU.add),
                  reads=[("tmp", t1), ("tmp", t2)], writes=[dkey])
            P.add("dve", lambda e: e.tensor_tensor(out=QT[64:128, i, 1, :], in0=tmp[t1][64:128, :], in1=tmp[t2][64:128, :], op=ALU.add),
                  reads=[("tmp", t1), ("tmp", t2)], writes=[dkey])
            return
        P.add("pool", lambda e: e.tensor_tensor(out=dst, in0=tmp[t1][:, :], in1=tmp[t2][:, :], op=ALU.add),
              reads=[("tmp", t1), ("tmp", t2)], writes=[dkey])

    def attention(layer, j, after_pair=None):
        DEPTH = 2
        if layer == 0:
            lo, bmax, mt, mtoff = max(0, 4 * j - 1), 1, m0, 3
        else:
            lo, bmax, mt, mtoff = max(0, 4 * j - 16), 16, m1, 3
        kbs = [4 * j] + [kb for kb in range(lo, 4 * j + 4) if kb != 4 * j]
        P.phase = 'attn%d' % layer
        items = []
        for c in range(4):
            X, Y = acc(), acc()
            for idx, kb in enumerate(kbs):
                b0 = 4 * j - kb
                r_lo, r_hi = max(0, -b0), min(4, bmax + 1 - b0)
                if idx == 0:
                    r_lo, r_hi = 0, 4
                items.append(dict(c=c, X=X, Y=Y, idx=idx, kb=kb, c0=r_lo * 128, c1=r_hi * 128,
                                  mcol=(b0 + mtoff) * 128, last=(idx == len(kbs) - 1)))

        def stage1(it):
            c, kb, c0, c1 = it["c"], it["kb"], it["c0"], it["c1"]
            if layer == 0:
                slot = kb % 8
                kT = K0T[:, slot * 128:(slot + 1) * 128]
                kkey = ("k0", (kb // 4) % 2)
            else:
                slot = kb % 20
                kT = K1T[:, c, slot * 128:(slot + 1) * 128]
                kkey = ("k1", (kb // 4) % 5, c)
            st["gp"] = (st.get("gp", 0) + 2) % 4
            b0_ = st["gp"]
            for hb in range(2):
                mm(ps[b0_ + hb][:, c0:c1], kT, QT[:, c, hb, c0:c1], True, True, [kkey, ("qt", c)], [("ps", b0_ + hb)])
            pt = npt()
            it["pt"] = pt
            src = psall[:, b0_ * T:(b0_ + 2) * T].rearrange("p (b n) -> p b n", b=2)
            P.add("act", lambda e: e.activation(
                out=Pt[pt][:, :, c0:c1], in_=src[:, :, c0:c1], func=AF.Exp, bias=kbias[:, kb:kb + 1], scale=0.125),
                reads=[("ps", b0_), ("ps", b0_ + 1), "kbias"], writes=[("pt", pt, 0), ("pt", pt, 1)])
            mcol = it["mcol"]
            for hb in range(2):
                P.add("pool" if hb == 0 else "dve", lambda e, hb=hb: e.tensor_tensor(
                    out=Pt[pt][:, hb, c0:c1], in0=Pt[pt][:, hb, c0:c1], in1=mt[:, mcol + c0:mcol + c1], op=ALU.mult),
                    reads=[("pt", pt, hb), "m0", "m1"], writes=[("pt", pt, hb)])

        def stage2(it):
            c, kb, c0, c1, pt = it["c"], it["kb"], it["c0"], it["c1"], it["pt"]
            for hb in range(2):
                if layer == 0:
                    slot = kb % 8
                    vT = V0a[:, slot, 0:128] if hb == 0 else V0a[:, slot, 64:192]
                    vkey = ("v0", (kb // 4) % 2)
                else:
                    slot = kb % 20
                    vT = V1a[:, slot, c * 192:c * 192 + 128] if hb == 0 else V1a[:, slot, c * 192 + 64:c * 192 + 192]
                    vkey = ("v1", (kb // 4) % 5)
                O = it["X"] if hb == 0 else it["Y"]
                mm(ps[O][:, c0:c1], vT, Pt[pt][:, hb, c0:c1], it["idx"] == 0, it["last"], [vkey, ("pt", pt, hb)], [("ps", O)])
            if it["last"]:
                normalise(c, it["X"], it["Y"])
                if after_pair and c in after_pair:
                    after_pair[c]()
                    P.phase = 'attn%d' % layer

        def normalise(c, X, Y):
            for hb in range(2):
                O = X if hb == 0 else Y
                dl, dh = (64, 128) if hb == 0 else (0, 64)
                ol, oh = (0, 64) if hb == 0 else (64, 128)
                t1 = ntmp()
                if layer == 0:
                    col = c if hb == 0 else 4 + c
                    P.add("act", lambda e, O=O, t1=t1, dl=dl, dh=dh, col=col: e.activation(
                        out=tmp[t1][dl:dh, :], in_=ps[O][dl:dh, :], func=AF.Ln, bias=esink[dl:dh, col:col + 1]),
                        reads=[("ps", O), "esink"], writes=[("tmp", t1)])
                else:
                    P.add("act", lambda e, O=O, t1=t1, dl=dl, dh=dh: e.activation(
                        out=tmp[t1][dl:dh, :], in_=ps[O][dl:dh, :], func=AF.Ln),
                        reads=[("ps", O)], writes=[("tmp", t1)])
                P.add("act", lambda e, t1=t1, dl=dl, dh=dh: e.activation(
                    out=tmp[t1][dl:dh, :], in_=tmp[t1][dl:dh, :], func=AF.Exp, scale=-1.0),
                    reads=[("tmp", t1)], writes=[("tmp", t1)])
                P.add("dve", lambda e, O=O, t1=t1, dl=dl, dh=dh, ol=ol, oh=oh, c=c: e.tensor_tensor(
                    out=hid[ol:oh, c, :], in0=ps[O][ol:oh, :], in1=tmp[t1][dl:dh, :], op=ALU.mult),
                    reads=[("ps", O), ("tmp", t1)], writes=[("hid", c)])

        n = len(items)
        for i in range(n + DEPTH):
            if i < n:
                stage1(items[i])
            if i - DEPTH >= 0:
                stage2(items[i - DEPTH])

    def out_proj(layer):
        P.phase = 'outproj'
        nb = acc()
        for o in range(4):
            slot = unit_ahead(("out", layer, o))
            w = W3(slot)
            for oc in range(2):
                bk = gen()
                for kc in range(KC):
                    mm(ps[bk][:, :], w[:, kc, oc * 128:(oc + 1) * 128], hid[:, kc, :], kc == 0, kc == KC - 1,
                       [("w", slot), ("hid", kc)], [("ps", bk)])
                f = 2 * o + oc
                P.add("dve", lambda e, bk=bk, f=f: e.tensor_tensor(out=h[:, f, :], in0=ps[bk][:, :], in1=h[:, f, :], op=ALU.add),
                      reads=[("ps", bk), ("h", f)], writes=[("h", f)])
                sq_part1(f)
                if f > 0:
                    sq_part2(nb, f - 1, f - 1 == 0, False)
        sq_part2(nb, 7, False, True)
        return nb

    def ffn(l, nb_in, want_sums=True):
        rmsnorm(VC["g_ffn0"] if l == 0 else VC["g_ffn1"], nb_in)
        P.phase = 'ffn_gu'
        for i in range(11):
            sg_ = unit_ahead(("gate", l, i))
            su_ = unit_ahead(("up", l, i))
            for oc in range(2):
                bg = proj_fm(sg_, oc)
                bu = proj_fm(su_, oc)
                t1 = ntmp()
                sigmoid_to(t1, ps[bg][:, :], 1.0, [("ps", bg)])
                P.add("dve", lambda e, bg=bg, t1=t1: e.tensor_tensor(out=tmp[t1][:, :], in0=ps[bg][:, :], in1=tmp[t1][:, :], op=ALU.mult),
                      reads=[("ps", bg), ("tmp", t1)], writes=[("tmp", t1)])
                f = 2 * i + oc
                P.add("dve", lambda e, bu=bu, t1=t1, f=f: e.tensor_tensor(out=hid[:, f, :], in0=ps[bu][:, :], in1=tmp[t1][:, :], op=ALU.mult),
                      reads=[("ps", bu), ("tmp", t1)], writes=[("hid", f)])
        P.phase = 'ffn_dn'
        nb = acc() if want_sums else None
        for oc in range(8):
            s0_ = unit_ahead(("down", l, oc, 0))
            s1_ = unit_ahead(("down", l, oc, 1))
            bk = gen()
            for kc in range(FC):
                slot = s0_ if kc < 11 else s1_
                w = W3d(slot)
                mm(ps[bk][:, :], w[:, kc % 11, :], hid[:, kc, :], kc == 0, kc == FC - 1, [("w", slot), ("hid", kc)], [("ps", bk)])
            P.add("dve", lambda e, bk=bk, oc=oc: e.tensor_tensor(out=h[:, oc, :], in0=ps[bk][:, :], in1=h[:, oc, :], op=ALU.add),
                  reads=[("ps", bk), ("h", oc)], writes=[("h", oc)])
            if want_sums:
                sq_part1(oc)
                if oc > 0:
                    sq_part2(nb, oc - 1, oc - 1 == 0, False)
        if want_sums:
            sq_part2(nb, 7, False, True)
        return nb

    def layer0(j, nb_in, light=False):
        csb = j % 2
        rmsnorm(VC["g_ev"], nb_in)
        P.phase = 'inproj0'
        def conv_part1():
            P.phase = 'conv'
            for cc in range(4):
                sA = unit_ahead(("conv", cc, 0))
                sB = unit_ahead(("conv", cc, 1))
                bk = gen()
                for tap in range(31):
                    slot = sA if tap < 16 else sB
                    w = W3c(slot)
                    mm(ps[bk][:, :], w[:, tap % 16, :], cpad[:, cc, tap:tap + T], tap == 0, tap == 30, [("w", slot), ("cpad", cc)], [("ps", bk)])
                P.add("act", lambda e, bk=bk, cc=cc: e.activation(out=cu[:, cc, :], in_=ps[bk][:, :], func=AF.Identity,
                                                                  bias=vecs[:, VC["conv_b"] + cc:VC["conv_b"] + cc + 1]),
                      reads=[("ps", bk), "vecs"], writes=[("cu", cc)])
                P.add("dve", lambda e, cc=cc: e.tensor_copy(out=hid[:, 8 + cc, :], in_=cu[:, cc, :]),
                      reads=[("cu", cc)], writes=[("hid", 8 + cc)])
                P.add("act", lambda e, cc=cc: e.activation(out=hid[:, 12 + cc, :], in_=cu[:, cc, :], func=AF.Square),
                      reads=[("cu", cc)], writes=[("hid", 12 + cc)])
                P.add("pool", lambda e, cc=cc: e.tensor_copy(out=cpad[:, cc, 0:30], in_=cpad[:, cc, T:T + 30]),
                      reads=[("cpad", cc)], writes=[("cpad", cc)])

        def conv_part2():
            P.phase = 'convln'
            b1 = acc()
            for cc in range(4):
                mm(ps[b1][:, :], ones[:, :], hid[:, 8 + cc, :], cc == 0, cc == 3, [("hid", 8 + cc), "ones"], [("ps", b1)])
            b2 = acc()
            for cc in range(4):
                mm(ps[b2][:, :], ones[:, :], hid[:, 12 + cc, :], cc == 0, cc == 3, [("hid", 12 + cc), "ones"], [("ps", b2)])
            tm = ntmp()
            P.add("act", lambda e: e.activation(out=tmp[tm][:, :], in_=ps[b1][:, :], func=AF.Identity, scale=1.0 / 512),
                  reads=[("ps", b1)], writes=[("tmp", tm)])
            tq = ntmp()
            P.add("dve", lambda e: e.tensor_tensor(out=tmp[tq][:, :], in0=tmp[tm][:, :], in1=tmp[tm][:, :], op=ALU.mult),
                  reads=[("tmp", tm)], writes=[("tmp", tq)])
            P.add("dve", lambda e: e.scalar_tensor_tensor(out=tmp[tq][:, :], in0=ps[b2][:, :], scalar=1.0 / 512, in1=tmp[tq][:, :],
                                                          op0=ALU.mult, op1=ALU.subtract),
                  reads=[("ps", b2), ("tmp", tq)], writes=[("tmp", tq)])
            P.add("act", lambda e: e.activation(out=tmp[tq][:, :], in_=tmp[tq][:, :], func=AF.Ln, bias=1e-5),
                  reads=[("tmp", tq)], writes=[("tmp", tq)])
            P.add("act", lambda e: e.activation(out=tmp[tq][:, :], in_=tmp[tq][:, :], func=AF.Exp, scale=-0.5),
                  reads=[("tmp", tq)], writes=[("tmp", tq)])
            for cc in range(4):
                P.add("dve", lambda e, cc=cc: e.tensor_tensor(out=cu[:, cc, :], in0=cu[:, cc, :], in1=tmp[tm][:, :], op=ALU.subtract),
                      reads=[("cu", cc), ("tmp", tm)], writes=[("cu", cc)])
                P.add("dve", lambda e, cc=cc: e.tensor_tensor(out=cu[:, cc, :], in0=cu[:, cc, :], in1=tmp[tq][:, :], op=ALU.mult),
                      reads=[("cu", cc), ("tmp", tq)], writes=[("cu", cc)])
                P.add("act", lambda e, cc=cc: e.activation(out=cu[:, cc, :], in_=cu[:, cc, :], func=AF.Identity,
                                                           scale=vecs[:, VC["ln_g"] + cc:VC["ln_g"] + cc + 1],
                                                           bias=vecs[:, VC["ln_b"] + cc:VC["ln_b"] + cc + 1]),
                      reads=[("cu", cc), "vecs"], writes=[("cu", cc)])

        silu_tmp = {}

        def conv_silu_a(cc):
            ts_ = ntmp()
            silu_tmp[cc] = ts_
            sigmoid_to(ts_, cu[:, cc, :], 1.0, [("cu", cc)])

        def conv_silu_b(cc):
            ts_ = silu_tmp[cc]
            P.add("dve", lambda e: e.tensor_tensor(out=hid[:, 4 + cc, :], in0=cu[:, cc, :], in1=tmp[ts_][:, :], op=ALU.mult),
                  reads=[("cu", cc), ("tmp", ts_)], writes=[("hid", 4 + cc)])

        for i in range(4):
            slot = unit_ahead(("glu", i))
            ba = proj_fm(slot, 0)
            bb = proj_fm(slot, 1)
            t1 = ntmp()
            sigmoid_to(t1, ps[bb][:, :], 1.0, [("ps", bb)])
            P.add("dve", lambda e, ba=ba, t1=t1, i=i: e.tensor_tensor(out=cpad[:, i, 30:30 + T], in0=ps[ba][:, :], in1=tmp[t1][:, :], op=ALU.mult),
                  reads=[("ps", ba), ("tmp", t1)], writes=[("cpad", i)])
        slot = unit_ahead(("k0",))
        bq = proj_fm(slot, 0)
        bs = proj_fm(slot, 1)
        par = j % 2
        rotary_to(K0T[:, par * 512:(par + 1) * 512], ("k0", par), bq, bs, csb)
        slot = unit_ahead(("v0",))
        w = W3(slot)
        bk = gen()
        for tb in range(4):
            for kc in range(KC):
                mm(ps[bk][:, tb * 128:(tb + 1) * 128], xn[:, kc, tb * 128:(tb + 1) * 128], w[:, kc, 0:128], kc == 0, kc == KC - 1,
                   [("w", slot), ("xn", kc)], [("ps", bk)])
        src = ps[bk][:, :].rearrange("p (b x) -> p b x", x=128)
        P.add("act", lambda e: e.copy(out=V0a[:, par * 4:par * 4 + 4, 0:64], in_=src[:, :, 0:64]),
              reads=[("ps", bk)], writes=[("v0", par)])
        P.add("dve", lambda e: e.tensor_copy(out=V0a[:, par * 4:par * 4 + 4, 128:192], in_=src[:, :, 64:128]),
              reads=[("ps", bk)], writes=[("v0", par)])
        if light:
            for cc in range(4):
                P.add("pool", lambda e, cc=cc: e.tensor_copy(out=cpad[:, cc, 0:30], in_=cpad[:, cc, T:T + 30]),
                      reads=[("cpad", cc)], writes=[("cpad", cc)])
            return None
        conv_part1()
        conv_part2()
        for i in range(4):
            conv_silu_a(i)
            slot = unit_ahead(("q0", i))
            bq = proj_fm(slot, 0)
            bs = proj_fm(slot, 1)
            rotary_to(None, ("qt", i), bq, bs, csb)
            conv_silu_b(i)
        attention(0, j)
        nb = out_proj(0)
        return ffn(0, nb)

    def layer1(j, own, nb_in):
        csb = j % 2
        rmsnorm(VC["g_od"], nb_in)
        P.phase = 'inproj1'
        r5 = j % 5
        stages = []
        gpos = [0]

        def g_advance(n):
            ph = P.phase
            P.phase = 'gchain'
            for _ in range(n):
                if gpos[0] < len(stages):
                    stages[gpos[0]]()
                    gpos[0] += 1
            P.phase = ph

        if own:
            sg_ = [unit_ahead(("g1", i)) for i in range(2)]
            gbank = []
            for tb in range(4):
                bk = acc()
                gbank.append(bk)
                for ug in range(2):
                    w = W3(sg_[ug])
                    for kc in range(KC):
                        mm(ps[bk][:, ug * 256:(ug + 1) * 256], xn[:, kc, tb * 128:(tb + 1) * 128], w[:, kc, :], kc == 0, kc == KC - 1,
                           [("w", sg_[ug]), ("xn", kc)], [("ps", bk)])
            def gstage(eng, fn_of_tb, reads_of_tb, writes_of_tb):
                def run():
                    for tb in range(4):
                        P.add(eng, (lambda e, tb=tb: fn_of_tb(e, tb)), reads=reads_of_tb(tb), writes=writes_of_tb(tb))
                return run
            G = lambda tb: ps[gbank[tb]]
            gk = lambda tb: ("ps", gbank[tb])
            tk = lambda tb: ("gt", tb)
            col = lambda tb: 16 + 10 * tb
            stages[:] = [
                gstage("act", lambda e, tb: e.activation(out=gts[tb][:, :], in_=G(tb)[:, :], func=AF.Square), lambda tb: [gk(tb)], lambda tb: [tk(tb)]),
                gstage("dve", lambda e, tb: e.tensor_scalar(out=gts[tb][:, :], in0=gts[tb][:, :], scalar1=0.044715, scalar2=1.0, op0=ALU.mult, op1=ALU.add),
                       lambda tb: [tk(tb)], lambda tb: [tk(tb)]),
                gstage("dve", lambda e, tb: e.tensor_tensor(out=gts[tb][:, :], in0=G(tb)[:, :], in1=gts[tb][:, :], op=ALU.mult),
                       lambda tb: [gk(tb), tk(tb)], lambda tb: [tk(tb)]),
                gstage("act", lambda e, tb: e.activation(out=gts[tb][:, :], in_=gts[tb][:, :], func=AF.Exp, scale=-GK2), lambda tb: [tk(tb)], lambda tb: [tk(tb)]),
                gstage("act", lambda e, tb: e.activation(out=gts[tb][:, :], in_=gts[tb][:, :], func=AF.Ln, bias=1.0), lambda tb: [tk(tb)], lambda tb: [tk(tb)]),
                gstage("act", lambda e, tb: e.activation(out=gts[tb][:, :], in_=gts[tb][:, :], func=AF.Exp, scale=-1.0), lambda tb: [tk(tb)], lambda tb: [tk(tb)]),
                gstage("dve", lambda e, tb: e.tensor_tensor(out=G(tb)[:, :], in0=G(tb)[:, :], in1=gts[tb][:, :], op=ALU.mult),
                       lambda tb: [gk(tb), tk(tb)], lambda tb: [gk(tb)]),
                gstage("dve", lambda e, tb: e.bn_stats(out=small[:, col(tb):col(tb) + 6], in_=G(tb)[:, :]), lambda tb: [gk(tb)], lambda tb: ["smg%da" % tb]),
                gstage("dve", lambda e, tb: e.bn_aggr(out=small[:, col(tb) + 6:col(tb) + 8], in_=small[:, col(tb):col(tb) + 6]),
                       lambda tb: ["smg%da" % tb], lambda tb: ["smg%db" % tb]),
                gstage("act", lambda e, tb: e.activation(out=small[:, col(tb) + 8:col(tb) + 9], in_=small[:, col(tb) + 7:col(tb) + 8], func=AF.Ln, bias=1e-5),
                       lambda tb: ["smg%db" % tb], lambda tb: ["smg%dc" % tb]),
                gstage("act", lambda e, tb: e.activation(out=small[:, col(tb) + 9:col(tb) + 10], in_=small[:, col(tb) + 8:col(tb) + 9], func=AF.Exp, scale=-0.5),
                       lambda tb: ["smg%dc" % tb], lambda tb: ["smg%dd" % tb]),
                gstage("dve", lambda e, tb: e.tensor_scalar(out=G(tb)[:, :], in0=G(tb)[:, :], scalar1=small[:, col(tb) + 6:col(tb) + 7],
                                                            scalar2=small[:, col(tb) + 9:col(tb) + 10], op0=ALU.subtract, op1=ALU.mult),
                       lambda tb: [gk(tb), "smg%db" % tb, "smg%dd" % tb], lambda tb: [gk(tb)]),
                gstage("dve", lambda e, tb: e.tensor_tensor(out=gts[tb][:, :], in0=G(tb)[:, :], in1=rep[:, 0:512], op=ALU.mult),
                       lambda tb: [gk(tb), "rep"], lambda tb: [tk(tb)]),
                gstage("pool", lambda e, tb: e.tensor_tensor(out=gn[:, tb, :], in0=gts[tb][:, :], in1=rep[:, 512:1024], op=ALU.add),
                       lambda tb: [tk(tb), "rep"], lambda tb: [("gn", tb)]),
            ]
        for i in range(4):
            slot = unit_ahead(("k1", i))
            bq = proj_fm(slot, 0)
            bs = proj_fm(slot, 1)
            rotary_to(K1T[:, i, r5 * 512:(r5 + 1) * 512], ("k1", r5, i), bq, bs, csb)
            g_advance(2)
        sv = [unit_ahead(("v1", i)) for i in range(2)]
        for tb in range(4):
            bk = gen()
            for uv in range(2):
                w = W3(sv[uv])
                for kc in range(KC):
                    mm(ps[bk][:, uv * 256:(uv + 1) * 256], xn[:, kc, tb * 128:(tb + 1) * 128], w[:, kc, :], kc == 0, kc == KC - 1,
                       [("w", sv[uv]), ("xn", kc)], [("ps", bk)])
            slot = r5 * 4 + tb
            dst = V1a[:, slot, :].rearrange("p (c x) -> p c x", x=192)
            src = ps[bk][:, :].rearrange("p (c x) -> p c x", x=128)
            P.add("act", lambda e, dst=dst, src=src: e.copy(out=dst[:, :, 0:64], in_=src[:, :, 0:64]),
                  reads=[("ps", bk)], writes=[("v1", r5)])
            P.add("dve", lambda e, dst=dst, src=src: e.tensor_copy(out=dst[:, :, 128:192], in_=src[:, :, 64:128]),
                  reads=[("ps", bk)], writes=[("v1", r5)])
            g_advance(1)
        if not own:
            return
        for i in range(2):
            slot = unit_ahead(("u1", i))
            for oc in range(2):
                bk = proj_fm(slot, oc)
                f = 2 * i + oc
                P.add("act", lambda e, bk=bk, f=f: e.copy(out=cu[:, f, :], in_=ps[bk][:, :]),
                      reads=[("ps", bk)], writes=[("cu", f)])
                g_advance(1)
        uk = lambda f: ("cu", f)
        tk2 = lambda f: ("gt", f)
        ustages = [
            gstage("act", lambda e, f: e.activation(out=gts[f][:, :], in_=cu[:, f, :], func=AF.Square), lambda f: [uk(f)], lambda f: [tk2(f)]),
            gstage("dve", lambda e, f: e.tensor_scalar(out=gts[f][:, :], in0=gts[f][:, :], scalar1=0.044715, scalar2=1.0, op0=ALU.mult, op1=ALU.add),
                   lambda f: [tk2(f)], lambda f: [tk2(f)]),
            gstage("dve", lambda e, f: e.tensor_tensor(out=gts[f][:, :], in0=cu[:, f, :], in1=gts[f][:, :], op=ALU.mult),
                   lambda f: [uk(f), tk2(f)], lambda f: [tk2(f)]),
            gstage("act", lambda e, f: e.activation(out=gts[f][:, :], in_=gts[f][:, :], func=AF.Exp, scale=-GK2), lambda f: [tk2(f)], lambda f: [tk2(f)]),
            gstage("act", lambda e, f: e.activation(out=gts[f][:, :], in_=gts[f][:, :], func=AF.Ln, bias=1.0), lambda f: [tk2(f)], lambda f: [tk2(f)]),
            gstage("act", lambda e, f: e.activation(out=gts[f][:, :], in_=gts[f][:, :], func=AF.Exp, scale=-1.0), lambda f: [tk2(f)], lambda f: [tk2(f)]),
            gstage("pool", lambda e, f: e.tensor_tensor(out=cu[:, f, :], in0=cu[:, f, :], in1=gts[f][:, :], op=ALU.mult),
                   lambda f: [uk(f), tk2(f)], lambda f: [uk(f)]),
        ]
        stages.extend(ustages)
        for i in range(4):
            slot = unit_ahead(("q1", i))
            bq = proj_fm(slot, 0)
            bs = proj_fm(slot, 1)
            rotary_to(None, ("qt", i), bq, bs, csb)
            g_advance(2)
        g_advance(len(stages))
        if j + 1 < nch:
            x_dma(j + 1, 0)
            x_dma(j + 1, 1)
        attention(1, j)
        P.phase = 'gmlp'
        for c in range(4):
            bk = gen()
            for tb in range(4):
                for h2 in range(2):
                    g = 2 * c + h2
                    mm(ps[bk][h2 * 64:(h2 + 1) * 64, tb * 128:(tb + 1) * 128], gn[:, tb, g * 64:(g + 1) * 64], wst[:, g, :], True, True,
                       [("gn", tb), "wst"], [("ps", bk)])
            t1 = ntmp()
            for tb in range(4):
                P.add("dve", lambda e, bk=bk, t1=t1, tb=tb, c=c: e.tensor_tensor(
                    out=tmp[t1][:, tb * 128:(tb + 1) * 128], in0=ps[bk][:, tb * 128:(tb + 1) * 128], in1=sbrep[:, c, :], op=ALU.add),
                    reads=[("ps", bk), "sbrep"], writes=[("tmp", t1)])
            P.add("pool", lambda e, t1=t1, c=c: e.tensor_tensor(out=hid[:, 4 + c, :], in0=tmp[t1][:, :], in1=cu[:, c, :], op=ALU.mult),
                  reads=[("tmp", t1), ("cu", c)], writes=[("hid", 4 + c)])
        nb = out_proj(1)
        ffn(1, nb, want_sums=False)

    def final_out(j):
        P.phase = 'final'
        oj = j - nhalo
        for tb in range(4):
            banks = []
            for half in range(2):
                bk = gen()
                banks.append(bk)
                for f in range(4):
                    fc = half * 4 + f
                    P.add("pe", lambda e, bk=bk, f=f, fc=fc, tb=tb: e.transpose(
                        ps[bk][:, f * 128:(f + 1) * 128], h[:, fc, tb * 128:(tb + 1) * 128], ident[:, :]),
                        reads=[("h", fc), "ident"], writes=[("ps", bk)])
                tj = ntmp()
                P.add("act", lambda e, bk=bk, tj=tj, half=half: e.activation(out=tmp[tj][:, :], in_=ps[bk][:, :], func=AF.Square,
                                                                             accum_out=small[:, 56 + half:57 + half]),
                      reads=[("ps", bk)], writes=[("tmp", tj), "sf%d" % half])
            P.add("dve", lambda e: e.tensor_tensor(out=small[:, 58:59], in0=small[:, 56:57], in1=small[:, 57:58], op=ALU.add),
                  reads=["sf0", "sf1"], writes=["sf2"])
            P.add("act", lambda e: e.activation(out=small[:, 59:60], in_=small[:, 58:59], func=AF.Ln, bias=1e-6, scale=1.0 / D),
                  reads=["sf2"], writes=["sf3"])
            P.add("act", lambda e: e.activation(out=small[:, 60:61], in_=small[:, 59:60], func=AF.Exp, scale=-0.5),
                  reads=["sf3"], writes=["sf4"])
            r0 = oj * T + tb * 128
            for half in range(2):
                bk = banks[half]
                ts_ = ntmp()
                P.add("dve", lambda e, bk=bk, half=half, ts_=ts_: e.scalar_tensor_tensor(
                    out=tmp[ts_][:, :], in0=ps[bk][:, :], scalar=small[:, 60:61],
                    in1=rep[:, 1024 + half * 512:1024 + (half + 1) * 512], op0=ALU.mult, op1=ALU.mult),
                    reads=[("ps", bk), "sf4", "rep"], writes=[("tmp", ts_)])
                P.add("pool", lambda e, r0=r0, half=half, ts_=ts_: e.dma_start(
                    out=out_d[r0:r0 + 128, half * 512:(half + 1) * 512], in_=tmp[ts_][:, :]),
                    reads=[("tmp", ts_)], dma="o%d" % ts_)

    cast_upto(CAST_AHEAD)
    load_cs(0)
    nb = load_x(0)
    for j in range(nch):
        own = j >= nhalo
        light = (j == 0 and nhalo >= 5)
        if j + 1 < nch:
            load_cs(j + 1)
        nb = layer0(j, nb, light)
        if j + 1 < nch and not own:
            x_dma(j + 1, 0)
            x_dma(j + 1, 1)
        if not light:
            layer1(j, own, nb)
        if own:
            final_out(j)
        if j + 1 < nch:
            nb = load_x(j + 1, prefetched=True)
    cast_upto(len(cast_order))
    P.emit()
    build_program.last_prog = P
    return nc


def _run(inputs, B, S, nhalo, runner=None):
    inp = {k: np.asarray(v, dtype=np.float32) for k, v in inputs.items()}
    half = S // 2
    nown = half // T
    nch = nown + nhalo
    wall = _build_wall(inp)
    consts = _build_consts(inp)
    x = inp["x"]
    in_maps = []
    for core in range(2 * B):
        b, hf = core // 2, core % 2
        s0 = hf * half
        xe = np.zeros((nch * T, D), np.float32)
        lo = s0 - nhalo * T
        src_lo = max(lo, 0)
        xe[src_lo - lo:, :] = x[b, src_lo:s0 + half, :]
        cs, kb = _core_tables(s0, nhalo, nch)
        m = dict(x=xe, wall=wall, cs=cs, kbias=kb)
        m.update(consts)
        in_maps.append(m)
    nc = build_program(nown, nhalo)
    if runner is None:
        res = run_bass_kernel_spmd(nc, in_maps, core_ids=list(range(2 * B)))
        results = res.results
    else:
        results = runner(nc, in_maps)
    out = np.zeros((B, S, D), np.float32)
    for core in range(2 * B):
        b, hf = core // 2, core % 2
        out[b, hf * half:(hf + 1) * half, :] = results[core]["out"]
    return out


def kernel(**inputs):
    return _run(inputs, B=4, S=8192, nhalo=5)
```
